# Optimizing a Trainium2 kernel written in Bass

```python
import math
import jax
import jax.numpy as jnp
from jax import lax
import numpy as np

D_MODEL = 2048
BATCH = 4
SEQ = 4096
DEPTH = 4

GRID_W = 64
CTX_LEN = 256
N_MOD = 6
EPS = 1e-6

NA_HEAD_DIM = 128
NA_W = D_MODEL // 2
NA_HEADS = NA_W // NA_HEAD_DIM
NA_ROWS = 8
NA_COLS = 16

SGU_W = D_MODEL // 4
SGU_HEADS = 4
SGU_HEAD_DIM = SGU_W // SGU_HEADS
SGU_CHUNK = 128

S5_W = D_MODEL // 4
S5_GROUP = 16
S5_GROUPS = S5_W // S5_GROUP
S5_STATE = 64
DT_MIN = 1e-3
DT_MAX = 1e-1

MIX_W = NA_W + SGU_W + S5_W
IN_W = 3 * NA_W + 2 * SGU_W + S5_W
COL_SPLITS = (NA_W, 2 * NA_W, 3 * NA_W, 3 * NA_W + SGU_W, 3 * NA_W + 2 * SGU_W)

PEER_HEADS = 8
PEER_QDIM = 256
PEER_NKEYS = 128
PEER_TOPK = 16
PEER_EXPERTS = PEER_NKEYS * PEER_NKEYS
PEER_CHUNK = 128

kernel_name = "hybrid_na_sgu_s5_peer_dit"

F32 = jnp.float32


def rms_norm(x, g):
    xf = x.astype(F32)
    y = xf * lax.rsqrt(jnp.mean(xf * xf, axis=-1, keepdims=True) + EPS)
    return (y * g.astype(F32)).astype(x.dtype)


def modulate(h, shift, scale):
    return h * (1 + scale) + shift


def gelu(x):
    return jax.nn.gelu(x, approximate=False)


def neighbourhood_attention(q, k, v, kc, vc, rpb):
    B, L, H, Dh = q.shape
    rows = L // GRID_W
    kr = min(NA_ROWS, rows)
    scale = Dh ** -0.5
    qg = q.reshape(B, rows, GRID_W, H, Dh)
    kg = k.reshape(B, rows, GRID_W, H, Dh)
    vg = v.reshape(B, rows, GRID_W, H, Dh)
    col = jnp.arange(GRID_W)
    col_start = jnp.clip(col - NA_COLS // 2, 0, GRID_W - NA_COLS)
    col_idx = col_start[:, None] + jnp.arange(NA_COLS)
    col_off = col_idx - col[:, None] + (NA_COLS - 1)

    def one_row(r):
        r_start = jnp.clip(r - kr // 2, 0, rows - kr)
        q_r = lax.dynamic_index_in_dim(qg, r, axis=1, keepdims=False)
        k_band = lax.dynamic_slice_in_dim(kg, r_start, kr, axis=1)
        v_band = lax.dynamic_slice_in_dim(vg, r_start, kr, axis=1)
        k_win = k_band[:, :, col_idx]
        v_win = v_band[:, :, col_idx]
        row_off = r_start + jnp.arange(kr) - r + (NA_ROWS - 1)
        bias = rpb[:, row_off[:, None, None], col_off[None, :, :]]
        bias = jnp.transpose(bias, (0, 2, 1, 3)).astype(F32)
        s_loc = jnp.einsum("bchd,brcwhd->bhcrw", q_r, k_win).astype(F32) * scale + bias[None]
        s_ctx = jnp.einsum("bchd,bnhd->bhcn", q_r, kc).astype(F32) * scale
        s = jnp.concatenate([s_loc.reshape(B, H, GRID_W, kr * NA_COLS), s_ctx], axis=-1)
        p = jax.nn.softmax(s, axis=-1)
        p_loc = p[..., : kr * NA_COLS].reshape(B, H, GRID_W, kr, NA_COLS).astype(v.dtype)
        p_ctx = p[..., kr * NA_COLS:].astype(v.dtype)
        return (jnp.einsum("bhcrw,brcwhd->bchd", p_loc, v_win)
                + jnp.einsum("bhcn,bnhd->bchd", p_ctx, vc))

    out = lax.map(one_row, jnp.arange(rows))
    return jnp.transpose(out, (1, 0, 2, 3, 4)).reshape(B, L, H * Dh)


def context_attention(qc, kc, vc):
    s = jnp.einsum("bqhd,bkhd->bhqk", qc, kc).astype(F32) * NA_HEAD_DIM ** -0.5
    p = jax.nn.softmax(s, axis=-1).astype(vc.dtype)
    o = jnp.einsum("bhqk,bkhd->bqhd", p, vc)
    return o.reshape(o.shape[0], o.shape[1], NA_W)


def spatial_gating(su, sv, w_s, b_s, g):
    B, L, _ = su.shape
    u = gelu(su)
    v = rms_norm(gelu(sv), g).reshape(B, L // SGU_CHUNK, SGU_CHUNK, SGU_HEADS, SGU_HEAD_DIM)
    mixed = jnp.einsum("hts,bnshc->bnthc", w_s, v) + b_s.T[:, :, None]
    return u * mixed.reshape(B, L, SGU_W)


def zoh_discretize(a_re, a_im, log_dt, b_re, b_im):
    lam = lax.complex(jnp.minimum(a_re.astype(F32), -1e-4), a_im.astype(F32))
    dt = jnp.exp(log_dt.astype(F32))[:, None]
    lam_bar = jnp.exp(lam * dt)
    b = lax.complex(b_re.astype(F32), b_im.astype(F32))
    b_bar = ((lam_bar - 1) / lam)[..., None] * b
    return lam_bar, b_bar


def _linear_recurrence(e1, e2):
    a1, b1 = e1
    a2, b2 = e2
    return a1 * a2, a2 * b1 + b2


def diag_scan(lam_bar, drive, h0, reverse):
    if h0 is not None:
        edge = drive.shape[1] - 1 if reverse else 0
        drive = drive.at[:, edge].add(lam_bar * h0)
    decay = jnp.broadcast_to(lam_bar, drive.shape)
    _, h = lax.associative_scan(_linear_recurrence, (decay, drive), reverse=reverse, axis=1)
    return h


def s5_mixer(s_in, s_ctx, a_re, a_im, log_dt, b_re, b_im, c_re, c_im, d, glu_w, glu_b, ctx_out):
    def to_groups(t):
        return t.astype(F32).reshape(t.shape[0], t.shape[1], S5_GROUPS, S5_GROUP)

    def drive(b_bar, u):
        return jnp.einsum("gph,blgh->blgp", b_bar, u.astype(jnp.complex64))

    def readout(c_mat, h):
        return jnp.real(jnp.einsum("ghp,blgp->blgh", c_mat, h))

    def glu(y):
        y = gelu(y.reshape(y.shape[0], y.shape[1], S5_W))
        return y * jax.nn.sigmoid(y @ glu_w.astype(F32) + glu_b.astype(F32))

    u, uc = to_groups(s_in), to_groups(s_ctx)
    d_g = d.astype(F32).reshape(S5_GROUPS, S5_GROUP)
    y = u * d_g
    yc = uc * d_g if ctx_out else None
    for direction in range(2):
        reverse = direction == 1
        lam_bar, b_bar = zoh_discretize(a_re[direction], a_im[direction], log_dt[direction],
                                        b_re[direction], b_im[direction])
        c_mat = lax.complex(c_re[direction].astype(F32), c_im[direction].astype(F32))
        h_ctx = diag_scan(lam_bar, drive(b_bar, uc), None, reverse)
        h_carry = h_ctx[:, 0] if reverse else h_ctx[:, -1]
        h_lat = diag_scan(lam_bar, drive(b_bar, u), h_carry, reverse)
        y = y + readout(c_mat, h_lat)
        if ctx_out:
            yc = yc + readout(c_mat, h_ctx)
    out = glu(y).astype(s_in.dtype)
    out_c = glu(yc).astype(s_in.dtype) if ctx_out else None
    return out, out_c


def merge_groups(attn, sgu, s5, out_norm, w_out):
    y = jnp.concatenate([
        rms_norm(attn, out_norm[:NA_W]),
        rms_norm(sgu, out_norm[NA_W:NA_W + SGU_W]),
        rms_norm(s5, out_norm[NA_W + SGU_W:]),
    ], axis=-1)
    return y @ w_out


def token_mixers(h, hc, w_in, w_out, out_norm, na_rpb, sgu_w, sgu_b, sgu_norm,
                 s5_a_re, s5_a_im, s5_log_dt, s5_b_re, s5_b_im, s5_c_re, s5_c_im,
                 s5_d, s5_glu_w, s5_glu_b, ctx_out):
    B, L, _ = h.shape

    def heads(t):
        return t.reshape(t.shape[0], t.shape[1], NA_HEADS, NA_HEAD_DIM)

    q, k, v, su, sv, s_in = jnp.split(h @ w_in, COL_SPLITS, axis=-1)
    if ctx_out:
        qc, kc, vc, suc, svc, s_inc = jnp.split(hc @ w_in, COL_SPLITS, axis=-1)
    else:
        kc, vc = jnp.split(hc @ w_in[:, NA_W:3 * NA_W], 2, axis=-1)
        s_inc = hc @ w_in[:, IN_W - S5_W:]
    kc_h, vc_h = heads(kc), heads(vc)

    attn = neighbourhood_attention(heads(q), heads(k), heads(v), kc_h, vc_h, na_rpb)
    sgu = spatial_gating(su, sv, sgu_w, sgu_b, sgu_norm)
    s5, s5c = s5_mixer(s_in, s_inc, s5_a_re, s5_a_im, s5_log_dt, s5_b_re, s5_b_im,
                       s5_c_re, s5_c_im, s5_d, s5_glu_w, s5_glu_b, ctx_out)
    y = merge_groups(attn, sgu, s5, out_norm, w_out)
    if not ctx_out:
        return y, None
    attn_c = context_attention(heads(qc), kc_h, vc_h)
    sgu_c = spatial_gating(suc, svc, sgu_w, sgu_b, sgu_norm)
    yc = merge_groups(attn_c, sgu_c, s5c, out_norm, w_out)
    return y, yc


def peer_ffn(h, w_q, k1, k2, u_tab, v_tab):
    B, L, D = h.shape
    tokens = h.reshape(B * L // PEER_CHUNK, PEER_CHUNK, D)
    half = PEER_QDIM // 2

    def block(xb):
        q = (xb @ w_q).reshape(PEER_CHUNK, PEER_HEADS, PEER_QDIM)
        s1 = jnp.einsum("thd,hkd->thk", q[..., :half], k1).astype(F32)
        s2 = jnp.einsum("thd,hkd->thk", q[..., half:], k2).astype(F32)
        v1, i1 = lax.top_k(s1, PEER_TOPK)
        v2, i2 = lax.top_k(s2, PEER_TOPK)
        cand_s = (v1[..., :, None] + v2[..., None, :]).reshape(PEER_CHUNK, PEER_HEADS, PEER_TOPK * PEER_TOPK)
        cand_e = (i1[..., :, None] * PEER_NKEYS + i2[..., None, :]).reshape(PEER_CHUNK, PEER_HEADS, PEER_TOPK * PEER_TOPK)
        best_s, pos = lax.top_k(cand_s, PEER_TOPK)
        expert = jnp.take_along_axis(cand_e, pos, axis=-1)
        gate = jax.nn.softmax(best_s, axis=-1)
        u = jnp.take(u_tab, expert, axis=0)
        act = gelu(jnp.einsum("thkd,td->thk", u, xb).astype(F32))
        w = (gate * act).astype(xb.dtype)
        return jnp.einsum("thk,thkd->td", w, jnp.take(v_tab, expert, axis=0))

    return lax.map(block, tokens).reshape(B, L, D)


def setup_inputs(seed: int = 0) -> dict:
    key = jax.random.key(seed)
    ks = jax.random.split(key, 32)
    D = D_MODEL
    G, P, H = S5_GROUPS, S5_STATE, S5_GROUP

    def nrm(k, shape, s):
        return jax.random.normal(k, shape, F32) * s

    return {
        "x": nrm(ks[0], (BATCH, SEQ, D), 1.0),
        "c": nrm(ks[1], (BATCH, D), 1.0),
        "ctx": nrm(ks[2], (BATCH, CTX_LEN, D), 1.0),
        "c_ctx": nrm(ks[3], (D,), 1.0),
        "ada_w": nrm(ks[4], (DEPTH, D, N_MOD * D), 0.5 * D ** -0.5),
        "ada_b": nrm(ks[5], (DEPTH, N_MOD * D), 0.01),
        "norm_mix": 1.0 + nrm(ks[6], (DEPTH, D), 0.01),
        "norm_ffn": 1.0 + nrm(ks[7], (DEPTH, D), 0.01),
        "w_in": nrm(ks[8], (DEPTH, D, IN_W), D ** -0.5),
        "w_out": nrm(ks[9], (DEPTH, MIX_W, D), MIX_W ** -0.5),
        "out_norm": 1.0 + nrm(ks[10], (DEPTH, MIX_W), 0.01),
        "na_rpb": nrm(ks[11], (DEPTH, NA_HEADS, 2 * NA_ROWS - 1, 2 * NA_COLS - 1), 0.02),
        "sgu_w": nrm(ks[12], (DEPTH, SGU_HEADS, SGU_CHUNK, SGU_CHUNK), 0.1 * SGU_CHUNK ** -0.5),
        "sgu_b": 1.0 + nrm(ks[13], (DEPTH, SGU_HEADS, SGU_CHUNK), 0.1),
        "sgu_norm": 1.0 + nrm(ks[14], (DEPTH, SGU_W), 0.01),
        "s5_a_re": -0.5 + nrm(ks[15], (DEPTH, 2, G, P), 0.01),
        "s5_a_im": math.pi * jnp.arange(P, dtype=F32) + nrm(ks[16], (DEPTH, 2, G, P), 0.01),
        "s5_log_dt": jax.random.uniform(ks[17], (DEPTH, 2, G), F32, math.log(DT_MIN), math.log(DT_MAX)),
        "s5_b_re": nrm(ks[18], (DEPTH, 2, G, P, H), (2 * H) ** -0.5),
        "s5_b_im": nrm(ks[19], (DEPTH, 2, G, P, H), (2 * H) ** -0.5),
        "s5_c_re": nrm(ks[20], (DEPTH, 2, G, H, P), (2 * P) ** -0.5),
        "s5_c_im": nrm(ks[21], (DEPTH, 2, G, H, P), (2 * P) ** -0.5),
        "s5_d": nrm(ks[22], (DEPTH, S5_W), 1.0),
        "s5_glu_w": nrm(ks[23], (DEPTH, S5_W, S5_W), S5_W ** -0.5),
        "s5_glu_b": nrm(ks[24], (DEPTH, S5_W), 0.01),
        "peer_wq": nrm(ks[25], (DEPTH, D, PEER_HEADS * PEER_QDIM), D ** -0.5),
        "peer_k1": nrm(ks[26], (DEPTH, PEER_HEADS, PEER_NKEYS, PEER_QDIM // 2), (PEER_QDIM // 2) ** -0.5),
        "peer_k2": nrm(ks[27], (DEPTH, PEER_HEADS, PEER_NKEYS, PEER_QDIM // 2), (PEER_QDIM // 2) ** -0.5),
        "peer_u": nrm(ks[28], (DEPTH, PEER_EXPERTS, D), D ** -0.5),
        "peer_v": nrm(ks[29], (DEPTH, PEER_EXPERTS, D), PEER_HEADS ** -0.5),
        "final_norm": 1.0 + nrm(ks[30], (D,), 0.01),
    }


def reference(x, c, ctx, c_ctx, ada_w, ada_b, norm_mix, norm_ffn, w_in, w_out, out_norm,
              na_rpb, sgu_w, sgu_b, sgu_norm, s5_a_re, s5_a_im, s5_log_dt, s5_b_re, s5_b_im,
              s5_c_re, s5_c_im, s5_d, s5_glu_w, s5_glu_b, peer_wq, peer_k1, peer_k2,
              peer_u, peer_v, final_norm):
    B = x.shape[0]
    D = x.shape[-1]
    sc = jax.nn.silu(c)
    scc = jax.nn.silu(c_ctx)
    xc = ctx
    for i in range(DEPTH):
        ctx_out = i < DEPTH - 1
        mod = (sc @ ada_w[i] + ada_b[i]).reshape(B, N_MOD, D)[:, :, None, :]
        modc = (scc @ ada_w[i] + ada_b[i]).reshape(N_MOD, D)

        h = modulate(rms_norm(x, norm_mix[i]), mod[:, 0], mod[:, 1])
        hc = modulate(rms_norm(xc, norm_mix[i]), modc[0], modc[1])
        y, yc = token_mixers(h, hc, w_in[i], w_out[i], out_norm[i], na_rpb[i],
                             sgu_w[i], sgu_b[i], sgu_norm[i],
                             s5_a_re[i], s5_a_im[i], s5_log_dt[i], s5_b_re[i], s5_b_im[i],
                             s5_c_re[i], s5_c_im[i], s5_d[i], s5_glu_w[i], s5_glu_b[i], ctx_out)
        x = x + mod[:, 2] * y

        h = modulate(rms_norm(x, norm_ffn[i]), mod[:, 3], mod[:, 4])
        x = x + mod[:, 5] * peer_ffn(h, peer_wq[i], peer_k1[i], peer_k2[i], peer_u[i], peer_v[i])

        if ctx_out:
            xc = xc + modc[2] * yc
            hc = modulate(rms_norm(xc, norm_ffn[i]), modc[3], modc[4])
            xc = xc + modc[5] * peer_ffn(hc, peer_wq[i], peer_k1[i], peer_k2[i], peer_u[i], peer_v[i])
    return rms_norm(x, final_norm)
```

```python
import math
from contextlib import ExitStack
import numpy as np
import concourse.bass as bass
import concourse.mybir as mybir
from concourse.bass_utils import run_bass_kernel_spmd

F32 = mybir.dt.float32
BF16 = mybir.dt.bfloat16
AF = mybir.ActivationFunctionType
ALU = mybir.AluOpType

SEM_LIMIT = 30000
N_DMA_SLOTS = 12


class T:
    __slots__ = ("w", "r")

    def __init__(self):
        self.w = None
        self.r = []


class Eng:
    def __init__(self, prog, name, handle):
        self.prog = prog
        self.name = name
        self.h = handle
        self.ops = []
        self.sems = []
        self.count = 0
        self.waited = {}
        self.dma_k = 0
        self.dma_slots = None

    def cur_token(self):
        k = self.count
        si = k // SEM_LIMIT
        while len(self.sems) <= si:
            self.sems.append(self.prog.nc.alloc_semaphore(f"s_{self.name}_{len(self.sems)}"))
        return (self.sems[si], k % SEM_LIMIT + 1, self.name)


class Prog:
    def __init__(self, nc):
        self.nc = nc
        self.E = {
            "pe": Eng(self, "pe", nc.tensor),
            "dve": Eng(self, "dve", nc.vector),
            "act": Eng(self, "act", nc.scalar),
            "pool": Eng(self, "pool", nc.gpsimd),
            "sp": Eng(self, "sp", nc.sync),
        }
        self.n_ops = 0
        self._tr = {}

    def sb(self, name, shape, dt):
        t = self.nc.alloc_sbuf_tensor(name, list(shape), dt)
        return t

    def _waits(self, eng, reads, writes, same_engine_sync):
        need = {}

        def add(tok):
            if tok is None:
                return
            sem, val, en = tok
            if en == eng.name and not same_engine_sync:
                return
            key = id(sem)
            if key not in need or need[key][1] < val:
                need[key] = (sem, val)

        for t in reads:
            add(t.w)
        for t in writes:
            add(t.w)
            for r in t.r:
                if r[2] != eng.name:
                    add(r)
        out = []
        for key, (sem, val) in need.items():
            if eng.waited.get(key, 0) >= val:
                continue
            eng.waited[key] = val
            out.append((sem, val))
        return out

    def _commit(self, tok, reads, writes):
        for t in reads:
            t.r.append(tok)
            if len(t.r) > 64:
                last = {}
                for r in t.r:
                    last[(id(r[0]))] = r if (id(r[0]) not in last or last[id(r[0])][1] < r[1]) else last[id(r[0])]
                t.r = list(last.values())
        for t in writes:
            t.w = tok
            t.r = []

    def op(self, en, fn, reads=(), writes=()):
        eng = self.E[en]
        sync_same = en in ("dve", "act", "pool")
        waits = self._waits(eng, reads, writes, sync_same)
        tok = eng.cur_token()
        eng.count += 1
        eng.ops.append((waits, fn, (tok[0], 1)))
        self._commit(tok, reads, writes)
        self.n_ops += 1
        return tok

    def I(self, en, mname, *args, reads=(), writes=(), **kw):
        def fn(h, mname=mname, args=args, kw=kw):
            return getattr(h, mname)(*args, **kw)
        return self.op(en, fn, reads, writes)

    def dma(self, q, out, in_, reads=(), writes=(), **kw):
        eng = self.E[q]
        if eng.dma_slots is None:
            eng.dma_slots = [[self.nc.alloc_semaphore(f"d_{q}_{i}"), 0] for i in range(N_DMA_SLOTS)]
        slot = eng.dma_slots[eng.dma_k % N_DMA_SLOTS]
        eng.dma_k += 1
        waits = self._waits(eng, reads, writes, True)
        sem = slot[0]
        if slot[1] > 0:
            key = id(sem)
            if eng.waited.get(key, 0) < slot[1]:
                eng.waited[key] = slot[1]
                waits.append((sem, slot[1]))
        slot[1] += 16
        tok = (sem, slot[1], "dma_" + q)

        def fn(h, out=out, in_=in_, kw=kw):
            return h.dma_start(out=out, in_=in_, **kw)
        eng.ops.append((waits, fn, (sem, 16)))
        self._commit(tok, reads, writes)
        self.n_ops += 1
        return tok

    def wait_all(self, en, trackers):
        eng = self.E[en]
        waits = self._waits(eng, trackers, (), True)
        eng.ops.append((waits, None, None))

    def barrier(self):
        toks = []
        for en, eng in self.E.items():
            if eng.count > 0:
                k = eng.count - 1
                toks.append((eng.sems[k // SEM_LIMIT], k % SEM_LIMIT + 1))
            if eng.dma_slots is not None and en != "pool":
                for sem, val in eng.dma_slots:
                    if val > 0:
                        toks.append((sem, val))
        for en, eng in self.E.items():
            waits = []
            for sem, val in toks:
                key = id(sem)
                if eng.waited.get(key, 0) >= val:
                    continue
                eng.waited[key] = val
                waits.append((sem, val))
            if waits:
                eng.ops.append((waits, None, None))

    def emit(self):
        nc = self.nc
        with nc.Block() as block:
            def mk(eng):
                ops = eng.ops
                eng.ops = []

                def body(h):
                    for waits, fn, inc in ops:
                        for sem, val in waits:
                            h.wait_ge(sem, val)
                        if fn is not None:
                            inst = fn(h)
                            inst.then_inc(inc[0], inc[1])
                return body
            if self.E["sp"].ops:
                block.sync(mk(self.E["sp"]))
            if self.E["pe"].ops:
                block.tensor(mk(self.E["pe"]))
            if self.E["dve"].ops:
                block.vector(mk(self.E["dve"]))
            if self.E["act"].ops:
                block.scalar(mk(self.E["act"]))
            if self.E["pool"].ops:
                block.gpsimd(mk(self.E["pool"]))

    def flush(self):
        self.barrier()
        self.emit()


D = 2048
CTX = 256
GRID_W = 64
NA_ROWS = 8
NA_COLS = 16
IN_W = 4608
NEXP = 16384
EPS = 1e-6
NEG = -30000.0
TS = 256


class Cfg:
    def __init__(self, L=4096, depth=4, debug=False, stop_after=None):
        self.L = L
        self.depth = depth
        self.T = L + CTX
        self.NST = self.T // TS
        self.rows = L // GRID_W
        self.debug = debug
        self.stop_after = stop_after


def na_patterns(rows):
    kr = min(NA_ROWS, rows)
    col = np.arange(GRID_W)
    col_start = np.clip(col - NA_COLS // 2, 0, GRID_W - NA_COLS)
    pats = {}
    plist = []
    info = []
    for qt in range(rows // 2):
        r0 = 2 * qt
        rs = [int(np.clip(r - kr // 2, 0, rows - kr)) for r in (r0, r0 + 1)]
        first = rs[0] // 2
        last = (rs[1] + kr - 1) // 2
        nch = last - first + 1
        assert nch <= 5
        idx = -np.ones((5, 128, 128), np.int64)
        for ci in range(nch):
            for kk in range(2):
                krow = 2 * (first + ci) + kk
                for qq in range(2):
                    r = r0 + qq
                    if not (rs[qq] <= krow < rs[qq] + kr):
                        continue
                    ro = krow - r + (NA_ROWS - 1)
                    for c in range(GRID_W):
                        cs = col_start[c]
                        kc = np.arange(cs, cs + NA_COLS)
                        co = kc - c + (NA_COLS - 1)
                        idx[ci, kk * 64 + kc, qq * 64 + c] = ro * (2 * NA_COLS - 1) + co
        key = idx.tobytes()
        if key not in pats:
            pats[key] = len(plist)
            plist.append(idx)
        info.append((first, nch, pats[key]))
    return info, np.stack(plist)


def prep_shared(inp, cfg):
    dep = cfg.depth
    f = lambda a: np.ascontiguousarray(a, dtype=np.float32)
    sh = {}
    sh["ada_w"] = f(inp["ada_w"][:dep])
    sh["ada_b"] = f(inp["ada_b"][:dep])
    sh["w_in"] = f(inp["w_in"][:dep])
    sh["w_out"] = f(inp["w_out"][:dep])
    sh["peer_wq"] = f(inp["peer_wq"][:dep])
    sh["peer_uT"] = f(np.transpose(inp["peer_u"][:dep], (0, 2, 1)))
    sh["peer_v"] = f(inp["peer_v"][:dep])
    sh["k1T"] = f(np.transpose(inp["peer_k1"][:dep], (0, 1, 3, 2)))
    sh["k2T"] = f(np.transpose(inp["peer_k2"][:dep], (0, 1, 3, 2)))
    sh["sgu_wT"] = f(np.transpose(inp["sgu_w"][:dep], (0, 1, 3, 2)))
    sh["sgu_b"] = f(inp["sgu_b"][:dep])
    sh["sgu_norm"] = f(inp["sgu_norm"][:dep])
    sh["glu_w"] = f(inp["s5_glu_w"][:dep])
    fm = lambda v: f(v.reshape(v.shape[0], -1, 128).transpose(0, 2, 1))
    sh["norm_mix_fm"] = fm(inp["norm_mix"][:dep])
    sh["norm_ffn_fm"] = fm(inp["norm_ffn"][:dep])
    sh["out_norm_fm"] = fm(inp["out_norm"][:dep])
    sh["s5_d_fm"] = fm(inp["s5_d"][:dep])
    sh["glu_b_fm"] = fm(inp["s5_glu_b"][:dep])
    sh["ada_b_fm"] = fm(inp["ada_b"][:dep])
    sh["final_norm"] = f(inp["final_norm"])
    info, pidx = na_patterns(cfg.rows)
    rpb = inp["na_rpb"][:dep].reshape(dep, 8, -1)
    tab = np.where(pidx[None, None] >= 0, rpb[:, :, np.clip(pidx, 0, None)], np.float32(NEG))
    sh["na_bias"] = f(tab)
    G, Pn, H = 32, 64, 16
    a_re = inp["s5_a_re"][:dep].reshape(dep, 2, 2048)
    a_im = inp["s5_a_im"][:dep].reshape(dep, 2, 2048)
    ldt = np.repeat(inp["s5_log_dt"][:dep], Pn, axis=-1)
    sm = lambda v: f(v.reshape(dep, 2, 16, 128).transpose(0, 1, 3, 2))
    sh["s5_are_sm"], sh["s5_aim_sm"], sh["s5_ldt_sm"] = sm(a_re), sm(a_im), sm(ldt)
    def ce(v):
        vt = v.reshape(dep, 2, 16, 128)
        o = np.repeat(vt, 32, axis=2)
        return f(o.reshape(dep, 2, 4, 128, 128).transpose(0, 1, 3, 2, 4))
    sh["s5_are_ce"], sh["s5_aim_ce"], sh["s5_ldt_ce"] = ce(a_re), ce(a_im), ce(ldt)

    def bT(b):
        o = np.zeros((dep, 2, 512, 128), np.float32)
        for g in range(G):
            for h in range(H):
                ch = g * 16 + h
                o[:, :, ch, (g % 2) * 64:(g % 2) * 64 + 64] = b[:, :, g, :, h]
        return f(o.reshape(dep, 2, 4, 128, 128).transpose(0, 1, 3, 2, 4))
    sh["s5_bre_ce"] = bT(inp["s5_b_re"][:dep])
    sh["s5_bim_ce"] = bT(inp["s5_b_im"][:dep])

    def cT(c):
        o = np.zeros((dep, 2, 128, 16, 128), np.float32)
        for g in range(G):
            i = g // 2
            q = i % 4
            for h in range(H):
                chl = 32 * q + (g % 2) * 16 + h
                o[:, :, (g % 2) * 64:(g % 2) * 64 + 64, i, chl] = c[:, :, g, h, :]
        return f(o)
    sh["s5_cre"] = cT(inp["s5_c_re"][:dep])
    sh["s5_cim"] = cT(inp["s5_c_im"][:dep])
    sh["ident"] = np.eye(128, dtype=np.float32)
    sh["pmask"] = f((np.arange(128)[:, None] // 32 == np.arange(4)[None, :]).astype(np.float32))
    return sh, info


def build(cfg, na_info, shapes):
    nc = bass.Bass("TRN2", target_bir_lowering=False)
    P = Prog(nc)
    dep, Tn, L, NST = cfg.depth, cfg.T, cfg.L, cfg.NST
    din = {}
    for name, shp in shapes.items():
        din[name] = nc.dram_tensor(name, list(shp), F32, kind="ExternalInput").ap()
    dbg = cfg.debug
    skind = "ExternalOutput" if dbg else "Internal"
    out_d = nc.dram_tensor("out", [L, D], F32, kind="ExternalOutput").ap()

    def scratch(name, shape, dt, always_internal=False):
        k = "Internal" if always_internal else skind
        return nc.dram_tensor(name, list(shape), dt, kind=k).ap()

    X = scratch("X", [Tn, D], F32)
    QT = scratch("QT", [1024, Tn], BF16)
    KT = scratch("KT", [1024, Tn], BF16)
    VV = scratch("VV", [Tn, 1024], BF16)
    SINT = scratch("SINT", [512, Tn], F32)
    ZT = scratch("ZT", [D, Tn], BF16)
    YS = scratch("YS", [2, 512, Tn], F32)
    GATE = scratch("GATE", [dep, 2, 2, 128, D], F32, always_internal=True)
    WIN = scratch("WINb", [dep, 9, 128, 16 * 512], BF16, True)
    WOUT = scratch("WOUTb", [dep, 4, 128, 16 * 512], BF16, True)
    WQ = scratch("WQb", [dep, 8, 128, 16 * 256], BF16, True)
    UT = scratch("UTb", [dep, 64, 128, 16 * 256], BF16, True)
    VT = scratch("VTb", [dep, NEXP, D], BF16, True)
    tX, tQT, tKT, tVV, tSINT, tZT, tYS, tGATE, tOUT = [T() for _ in range(9)]
    tXs = [T() for _ in range(NST)]
    tW = {}

    ident_f = P.sb("ident_f", [128, 128], F32)
    ident_b = P.sb("ident_b", [128, 128], BF16)
    ones_b = P.sb("ones_b", [128, 128], BF16)
    eps_t = P.sb("eps_t", [128, 1], F32)
    tC = T()
    P.dma("sp", ident_f[:], din["ident"], writes=[tC])
    P.op("dve", lambda h: h.tensor_copy(out=ident_b[:], in_=ident_f[:]), reads=[tC], writes=[tC])
    P.op("dve", lambda h: h.memset(ones_b[:], 1.0), writes=[tC])
    P.op("dve", lambda h: h.memset(eps_t[:], EPS), writes=[tC])
    pmask = P.sb("pmask_sb", [128, 4], F32)
    P.dma("sp", pmask[:], din["pmask"], writes=[tC])

    PS = [nc.alloc_psum_tensor(f"ps{i}", [128, 512], F32) for i in range(8)]
    tPS = [T() for _ in range(8)]

    def cast_weights(l):
        for name, src, dst, ng, w in (("win", din["w_in"], WIN, 9, 512), ("wout", din["w_out"], WOUT, 4, 512),
                                     ("wq", din["peer_wq"], WQ, 8, 256), ("ut", din["peer_uT"], UT, 64, 256)):
            for g in range(ng):
                t = T()
                tW[(name, l, g)] = t
                P.dma("pool", dst[l, g].rearrange("p (k n) -> p k n", n=w), src[l, :, g * w:(g + 1) * w].rearrange("(k p) n -> p k n", p=128), writes=[t])
        t = T()
        tW[("vt", l)] = t
        for r0 in range(0, NEXP, 256):
            P.dma("pool", VT[l, r0:r0 + 256, :], din["peer_v"][l, r0:r0 + 256, :], writes=[t])

    for r0 in range(0, Tn, 256):
        P.dma("sp", X[r0:r0 + 256, :], din["x_in"][r0:r0 + 256, :], writes=[tXs[r0 // 256]])

    for l in range(dep):
        cast_weights(l)

    modfm = P.sb("modfm", [128, dep, 96, 2], F32)
    Amix = P.sb("Amix", [128, dep, 16, 2], F32)
    Affn = P.sb("Affn", [128, dep, 16, 2], F32)
    tMOD = T()
    scT = P.sb("scT", [128, 16, 2], F32)
    tSC = T()
    for r_ in range(2):
        P.dma("sp", scT[:, :, r_], din["cc"][r_].rearrange("(k p) -> p k", p=128), writes=[tSC], allow_slow_non_contiguous=True)
    sig = P.sb("sig_sc", [128, 16, 2], F32)
    P.op("act", lambda h: h.activation(out=sig[:], in_=scT[:], func=AF.Silu), reads=[tSC], writes=[tSC])
    scS = sig
    vec_fm = {}
    for nm in ("norm_mix_fm", "norm_ffn_fm", "out_norm_fm", "ada_b_fm", "s5_d_fm", "glu_b_fm"):
        shp = shapes[nm]
        tt = P.sb("v_" + nm, [128, dep, shp[2]], F32)
        P.dma("sp", tt[:], din[nm].rearrange("l p k -> p l k"), writes=[tMOD])
        vec_fm[nm] = tt

    with ExitStack() as es:
        aw0 = es.enter_context(nc.sbuf_tensor("adaw_a", [128, 16, 128], F32))
        aw1 = es.enter_context(nc.sbuf_tensor("adaw_b", [128, 16, 128], F32))
        ag0 = es.enter_context(nc.sbuf_tensor("adag_a", [128, 16, 512], F32))
        ag1 = es.enter_context(nc.sbuf_tensor("adag_b", [128, 16, 512], F32))
        gtmp = es.enter_context(nc.sbuf_tensor("gtmp", [128, 512], F32))
        gbias = es.enter_context(nc.sbuf_tensor("gbias", [128, 512], F32))
        aws = [aw0, aw1]
        taw = [T(), T()]
        ags = [ag0, ag1]
        tag = [T(), T()]
        tgt, tgb = T(), T()
        k = 0
        for l in range(dep):
            for j in range(96):
                b = k % 2
                k += 1
                P.dma("sp", aws[b][:], din["ada_w"][l, :, j * 128:(j + 1) * 128].rearrange("(k p) n -> p k n", p=128),
                      writes=[taw[b]])
                bank = k % 2
                for kk in range(16):
                    P.op("pe", lambda h, b=b, kk=kk, bank=bank: h.matmul(PS[bank][:, 0:2], lhsT=aws[b][:, kk, :], rhs=scS[:, kk, :],
                                                                       start=(kk == 0), stop=(kk == 15)),
                         reads=[taw[b], tSC], writes=[tPS[bank]])
                P.op("dve", lambda h, l=l, j=j, bank=bank: h.tensor_scalar(out=modfm[:, l, j, :], in0=PS[bank][:, 0:2],
                                                                          scalar1=vec_fm["ada_b_fm"][:, l, j:j + 1], scalar2=None,
                                                                          op0=ALU.add),
                     reads=[tPS[bank], tMOD], writes=[tMOD])
            for (Aout, gname, sidx) in ((Amix, "norm_mix_fm", 1), (Affn, "norm_ffn_fm", 4)):
                for r in range(2):
                    P.op("dve", lambda h, Aout=Aout, gname=gname, sidx=sidx, r=r, l=l: h.scalar_tensor_tensor(
                        out=Aout[:, l, :, r], in0=modfm[:, l, sidx * 16:(sidx + 1) * 16, r], scalar=1.0,
                        in1=vec_fm[gname][:, l, :], op0=ALU.add, op1=ALU.mult), reads=[tMOD], writes=[tMOD])
            kg = 0
            for which, gidx in ((0, 2), (1, 5)):
                for cg in range(4):
                    b = kg % 2
                    kg += 1
                    c0 = gidx * D + cg * 512
                    P.dma("sp", ags[b][:], din["ada_w"][l, :, c0:c0 + 512].rearrange("(k p) n -> p k n", p=128), writes=[tag[b]])
                    P.dma("sp", gbias[:], din["ada_b"][l, c0:c0 + 512].partition_broadcast(128), writes=[tgb])
                    for r in range(2):
                        bank = 2 + (r % 2)
                        for kk in range(16):
                            P.op("pe", lambda h, b=b, kk=kk, r=r, bank=bank: h.matmul(
                                PS[bank][:, :], lhsT=scS[:, kk, r:r + 1].to_broadcast([128, 128]), rhs=ags[b][:, kk, :],
                                start=(kk == 0), stop=(kk == 15)), reads=[tag[b], tSC], writes=[tPS[bank]])
                        P.op("dve", lambda h, bank=bank: h.tensor_tensor(out=gtmp[:], in0=PS[bank][:, :], in1=gbias[:], op=ALU.add),
                             reads=[tPS[bank], tgb], writes=[tgt])
                        P.dma("sp", GATE[l, which, r, :, cg * 512:(cg + 1) * 512], gtmp[:], reads=[tgt], writes=[tGATE])
        P.flush()

    def norm_modT(xt, txt, A_ap, B_ap, hT, thT, tsl, tmp):
        junk, ss, rt, xn, tj = tmp
        P.op("act", lambda h: h.activation(out=junk[:], in_=xt[:], func=AF.Square, accum_out=ss[:, 0:1]), reads=[txt], writes=[tj])
        P.op("act", lambda h: h.activation(out=rt[:], in_=ss[:], func=AF.Sqrt, scale=1.0 / D, bias=eps_t[:, 0:1]), reads=[tj, tC], writes=[tj])
        P.op("dve", lambda h: h.reciprocal(out=rt[:], in_=rt[:]), reads=[tj], writes=[tj])
        P.op("act", lambda h: h.activation(out=xn[:], in_=xt[:], func=AF.Copy, scale=rt[:, 0:1]), reads=[txt, tj], writes=[tj])
        for c4 in range(4):
            bank = 4 + (c4 % 2)
            psb = PS[bank][:].bitcast(BF16)
            for j in range(4):
                c = c4 * 4 + j
                P.op("pe", lambda h, c=c, j=j, psb=psb: h.transpose(out=psb[:, j * 128:(j + 1) * 128], in_=xn[:, c * 128:(c + 1) * 128],
                                                                    identity=ident_b[:]), reads=[tj, tC], writes=[tPS[bank]])
            for j in range(4):
                c = c4 * 4 + j
                eng = "dve" if j % 2 == 0 else "act"
                if eng == "dve":
                    P.I("dve", "tensor_scalar", out=hT[:, c, tsl], in0=psb[:, j * 128:(j + 1) * 128],
                        scalar1=A_ap(c), scalar2=B_ap(c), op0=ALU.mult, op1=ALU.add,
                        reads=[tPS[bank], tMOD], writes=[thT])
                else:
                    P.I("act", "activation", out=hT[:, c, tsl], in_=psb[:, j * 128:(j + 1) * 128],
                        func=AF.Identity, scale=A_ap(c), bias=B_ap(c),
                        reads=[tPS[bank], tMOD], writes=[thT])

    def rstd_bc_from_psum(bank, nfeat, dst, tdst, width):
        P.op("act", lambda h: h.activation(out=dst[:, 0:width], in_=PS[bank][:, 0:width], func=AF.Sqrt, scale=1.0 / nfeat, bias=eps_t[:, 0:1]),
             reads=[tPS[bank], tC], writes=[tdst])
        P.op("dve", lambda h: h.reciprocal(out=dst[:, 0:width], in_=dst[:, 0:width]), reads=[tdst], writes=[tdst])

    for l in range(dep):
        with ExitStack() as es:
            xa0 = es.enter_context(nc.sbuf_tensor(f"xa0_{l}", [128, D], F32))
            xa1 = es.enter_context(nc.sbuf_tensor(f"xa1_{l}", [128, D], F32))
            ss = es.enter_context(nc.sbuf_tensor(f"ssA_{l}", [128, 1], F32))
            rt = es.enter_context(nc.sbuf_tensor(f"rtA_{l}", [128, 1], F32))
            xn = es.enter_context(nc.sbuf_tensor(f"xnA_{l}", [128, D], BF16))
            hT = es.enter_context(nc.sbuf_tensor(f"hTA_{l}", [128, 16, TS], BF16))
            wA0 = es.enter_context(nc.sbuf_tensor(f"wA0_{l}", [128, 16, 512], BF16))
            wA1 = es.enter_context(nc.sbuf_tensor(f"wA1_{l}", [128, 16, 512], BF16))
            wA2 = es.enter_context(nc.sbuf_tensor(f"wA2_{l}", [128, 16, 512], BF16))
            evb = es.enter_context(nc.sbuf_tensor(f"evb_{l}", [128, 4, TS], BF16))
            evf = es.enter_context(nc.sbuf_tensor(f"evf_{l}", [128, 4, TS], F32))
            uT = es.enter_context(nc.sbuf_tensor(f"uT_{l}", [128, 4, TS], F32))
            vtok = es.enter_context(nc.sbuf_tensor(f"vtok_{l}", [128, 2, 1024], BF16))
            svg = es.enter_context(nc.sbuf_tensor(f"svg_{l}", [128, 512], F32))
            svq = es.enter_context(nc.sbuf_tensor(f"svq_{l}", [128, 512], F32))
            vsg = es.enter_context(nc.sbuf_tensor(f"vsg_{l}", [128, 2, 512], BF16))
            gsgu = es.enter_context(nc.sbuf_tensor(f"gsgu_{l}", [128, 512], F32))
            bsb = es.enter_context(nc.sbuf_tensor(f"bsb_{l}", [128, 4, 128], F32))
            wsT = es.enter_context(nc.sbuf_tensor(f"wsT_{l}", [128, 4, 128], BF16))
            wsTf = es.enter_context(nc.sbuf_tensor(f"wsTf_{l}", [128, 4, 128], F32))
            sguT = es.enter_context(nc.sbuf_tensor(f"sguT_{l}", [128, 4, 128], F32))
            sqb = es.enter_context(nc.sbuf_tensor(f"sqb_{l}", [128, 4, 128], BF16))
            rsb = es.enter_context(nc.sbuf_tensor(f"rsb_{l}", [128, 128], F32))
            zsg = es.enter_context(nc.sbuf_tensor(f"zsg_{l}", [128, 4, 128], BF16))
            xa = [xa0, xa1]
            txa = [T(), T()]
            thT = T()
            wA = [wA0, wA1, wA2]
            twA = [(T(), T()) for _ in range(3)]
            tev, tevf, tuT, tvtok, tsv, tvsg, tsguc, tsguT, tsq, trs, tzs = [T() for _ in range(11)]
            tmp = (xn, ss, rt, xn, T())
            P.dma("sp", gsgu[:], din["sgu_norm"][l].partition_broadcast(128), writes=[tsguc])
            P.dma("sp", bsb[:], din["sgu_b"][l].partition_broadcast(128), writes=[tsguc])
            P.dma("sp", wsTf[:], din["sgu_wT"][l].rearrange("h s t -> s h t"), writes=[tsguc])
            P.op("dve", lambda h: h.tensor_copy(out=wsT[:], in_=wsTf[:]), reads=[tsguc], writes=[tsguc])
            wk = 0
            for st in range(NST):
                r = 1 if st == 0 else 0
                t0 = st * TS
                for ti in range(2):
                    P.dma("sp", xa[ti][:], X[t0 + ti * 128:t0 + (ti + 1) * 128, :], reads=[tXs[st]], writes=[txa[ti]])
                    norm_modT(xa[ti], txa[ti], lambda c: Amix[:, l, c, r:r + 1], lambda c: modfm[:, l, 0 * 16 + c, r:r + 1],
                              hT, thT, slice(ti * 128, (ti + 1) * 128), tmp)
                for g in range(9):
                    b = wk % 3
                    wk += 1
                    wsrc = WIN[l, g].rearrange("p (k n) -> p k n", n=512)
                    P.dma("act", wA[b][:, 0:8, :], wsrc[:, 0:8, :], reads=[tW[("win", l, g)]], writes=[twA[b][0]])
                    P.dma("sp", wA[b][:, 8:16, :], wsrc[:, 8:16, :], reads=[tW[("win", l, g)]], writes=[twA[b][1]])
                    if g in (0, 1, 2, 3, 6, 8):
                        for blk in range(4):
                            bank = blk % 2
                            for kk in range(16):
                                P.op("pe", lambda h, b=b, kk=kk, blk=blk, bank=bank: h.matmul(
                                    PS[bank][:, 0:TS], lhsT=wA[b][:, kk, blk * 128:(blk + 1) * 128], rhs=hT[:, kk, :],
                                    start=(kk == 0), stop=(kk == 15)), reads=[twA[b][kk // 8], thT], writes=[tPS[bank]])
                            if g in (0, 1, 2, 3):
                                P.op("act", lambda h, blk=blk, bank=bank: h.copy(out=evb[:, blk, :], in_=PS[bank][:, 0:TS]),
                                     reads=[tPS[bank]], writes=[tev])
                            elif g == 6:
                                P.op("act", lambda h, blk=blk, bank=bank: h.activation(out=uT[:, blk, :], in_=PS[bank][:, 0:TS], func=AF.Gelu),
                                     reads=[tPS[bank]], writes=[tuT])
                            else:
                                P.op("act", lambda h, blk=blk, bank=bank: h.copy(out=evf[:, blk, :], in_=PS[bank][:, 0:TS]),
                                     reads=[tPS[bank]], writes=[tevf])
                        if g in (0, 1):
                            P.dma("sp", QT[g * 512:(g + 1) * 512, t0:t0 + TS].rearrange("(b p) t -> p b t", p=128), evb[:],
                                  reads=[tev], writes=[tQT])
                        elif g in (2, 3):
                            P.dma("sp", KT[(g - 2) * 512:(g - 1) * 512, t0:t0 + TS].rearrange("(b p) t -> p b t", p=128), evb[:],
                                  reads=[tev], writes=[tKT])
                        elif g == 8:
                            P.dma("sp", SINT[:, t0:t0 + TS].rearrange("(b p) t -> p b t", p=128), evf[:], reads=[tevf], writes=[tSINT])
                    else:
                        for ti in range(2):
                            bank = 2 + ti
                            for kk in range(16):
                                P.op("pe", lambda h, b=b, kk=kk, ti=ti, bank=bank: h.matmul(
                                    PS[bank][:, :], lhsT=hT[:, kk, ti * 128:(ti + 1) * 128], rhs=wA[b][:, kk, :],
                                    start=(kk == 0), stop=(kk == 15)), reads=[twA[b][kk // 8], thT], writes=[tPS[bank]])
                            if g in (4, 5):
                                P.op("act", lambda h, ti=ti, bank=bank, g=g: h.copy(out=vtok[:, ti, (g - 4) * 512:(g - 3) * 512], in_=PS[bank][:, :]),
                                     reads=[tPS[bank]], writes=[tvtok])
                            else:
                                P.op("act", lambda h, bank=bank: h.activation(out=svg[:], in_=PS[bank][:, :], func=AF.Gelu),
                                     reads=[tPS[bank]], writes=[tsv])
                                P.op("act", lambda h: h.activation(out=svq[:], in_=svg[:], func=AF.Square, accum_out=ss[:, 0:1]),
                                     reads=[tsv], writes=[tsv])
                                P.op("act", lambda h: h.activation(out=rt[:], in_=ss[:], func=AF.Sqrt, scale=1.0 / 512, bias=eps_t[:, 0:1]),
                                     reads=[tsv, tC], writes=[tsv])
                                P.op("dve", lambda h: h.reciprocal(out=rt[:], in_=rt[:]), reads=[tsv], writes=[tsv])
                                P.op("dve", lambda h, ti=ti: h.scalar_tensor_tensor(out=vsg[:, ti, :], in0=svg[:], scalar=rt[:, 0:1], in1=gsgu[:],
                                                                                   op0=ALU.mult, op1=ALU.mult), reads=[tsv, tsguc], writes=[tvsg])
                        if g == 5:
                            P.dma("sp", VV[t0:t0 + TS, :].rearrange("(i p) n -> p i n", p=128), vtok[:], reads=[tvtok], writes=[tVV])
                for ti in range(2):
                    for hh in range(4):
                        bank = 6 + (hh % 2)
                        P.op("pe", lambda h, ti=ti, hh=hh, bank=bank: h.matmul(PS[bank][:, 0:128], lhsT=vsg[:, ti, hh * 128:(hh + 1) * 128],
                                                                              rhs=wsT[:, hh, :], start=True, stop=True),
                             reads=[tvsg, tsguc], writes=[tPS[bank]])
                        P.op("dve", lambda h, hh=hh, bank=bank: h.tensor_tensor(out=sguT[:, hh, :], in0=PS[bank][:, 0:128], in1=bsb[:, hh, :], op=ALU.add),
                             reads=[tPS[bank], tsguc], writes=[tsguT])
                        P.op("dve", lambda h, hh=hh, ti=ti: h.tensor_tensor(out=sguT[:, hh, :], in0=sguT[:, hh, :], in1=uT[:, hh, ti * 128:(ti + 1) * 128],
                                                                           op=ALU.mult), reads=[tsguT, tuT], writes=[tsguT])
                        P.op("act", lambda h, hh=hh: h.activation(out=sqb[:, hh, :], in_=sguT[:, hh, :], func=AF.Square), reads=[tsguT], writes=[tsq])
                    for hh in range(4):
                        P.op("pe", lambda h, hh=hh: h.matmul(PS[1][:, 0:128], lhsT=ones_b[:], rhs=sqb[:, hh, :], start=(hh == 0), stop=(hh == 3)),
                             reads=[tsq, tC], writes=[tPS[1]])
                    rstd_bc_from_psum(1, 512, rsb, trs, 128)
                    for hh in range(4):
                        P.op("dve", lambda h, hh=hh: h.scalar_tensor_tensor(out=zsg[:, hh, :], in0=sguT[:, hh, :],
                                                                            scalar=vec_fm["out_norm_fm"][:, l, 8 + hh:9 + hh], in1=rsb[:],
                                                                            op0=ALU.mult, op1=ALU.mult), reads=[tsguT, trs, tMOD], writes=[tzs])
                    tt0 = t0 + ti * 128
                    P.dma("sp", ZT[1024:1536, tt0:tt0 + 128].rearrange("(b p) t -> p b t", p=128), zsg[:], reads=[tzs], writes=[tZT])
            P.flush()
        if cfg.stop_after == ("A", l):
            break
        with ExitStack() as es:
            S_ = lambda n, shp, dt: es.enter_context(nc.sbuf_tensor(f"{n}_{l}", shp, dt))
            NT = Tn // 128
            KTa = S_("KTa", [128, 8, Tn], BF16)
            Va = S_("Va", [128, NT, 1024], BF16)
            Qt = [S_("Qt0", [128, 8, 128], BF16), S_("Qt1", [128, 8, 128], BF16)]
            bT = [S_("bT0", [128, 5, 128], F32), S_("bT1", [128, 5, 128], F32)]
            tmpS = [S_("tmpS0", [128, 5, 128], F32), S_("tmpS1", [128, 5, 128], F32)]
            PT = [S_("PT0", [128, 7, 128], BF16), S_("PT1", [128, 7, 128], BF16)]
            rden = S_("rden", [128, 128], F32)
            attnT = S_("attnT", [128, 8, 128], F32)
            sqA = S_("sqA", [128, 8, 128], BF16)
            rsA = S_("rsA", [128, 128], F32)
            zA = S_("zA", [128, 8, 128], BF16)
            tKTa, tVa, trden, tattn, tsqA, trsA, tzA = [T() for _ in range(7)]
            tQt, tbT, ttmpS, tPT = [T(), T()], [T(), T()], [T(), T()], [T(), T()]
            for h_ in range(8):
                P.dma("sp", KTa[:, h_, :], KT[h_ * 128:(h_ + 1) * 128, :], reads=[tKT], writes=[tKTa])
            for i_ in range(NT):
                P.dma("act", Va[:, i_, :], VV[i_ * 128:(i_ + 1) * 128, :], reads=[tVV], writes=[tVa])
            sc_ = 128.0 ** -0.5
            it = 0
            qtiles = ([] if l == dep - 1 else [("ctx", 0), ("ctx", 1)]) + [("lat", q) for q in range(L // 128)]
            for qi, (kind, qn) in enumerate(qtiles):
                qb = qi % 2
                qtok = qn * 128 if kind == "ctx" else CTX + qn * 128
                P.dma("sp", Qt[qb][:], QT[:, qtok:qtok + 128].rearrange("(h d) t -> d h t", d=128), reads=[tQT], writes=[tQt[qb]])
                if kind == "lat":
                    first, nch, pat = na_info[qn]
                else:
                    first, nch, pat = 0, 0, 0
                for h_ in range(8):
                    p = it % 2
                    it += 1
                    bA, bB, bC = PS[2 * p], PS[2 * p + 1], PS[4 + p]
                    tA_, tB_, tC_ = tPS[2 * p], tPS[2 * p + 1], tPS[4 + p]
                    if nch:
                        P.dma("sp", bT[p][:, 0:nch, :], din["na_bias"][l, h_, pat, 0:nch].rearrange("c k q -> k c q"), writes=[tbT[p]])
                    for ci in range(nch):
                        kt0 = CTX + (first + ci) * 128
                        dst = bA[:, ci * 128:(ci + 1) * 128] if ci < 4 else bB[:, 0:128]
                        P.I("pe", "matmul", dst, lhsT=KTa[:, h_, kt0:kt0 + 128], rhs=Qt[qb][:, h_, :], start=True, stop=True,
                            reads=[tKTa, tQt[qb]], writes=[tA_ if ci < 4 else tB_])
                    for cc_ in range(2):
                        P.I("pe", "matmul", bB[:, 128 + cc_ * 128:256 + cc_ * 128], lhsT=KTa[:, h_, cc_ * 128:(cc_ + 1) * 128], rhs=Qt[qb][:, h_, :],
                            start=True, stop=True, reads=[tKTa, tQt[qb]], writes=[tB_])
                    if nch:
                        nA = min(nch, 4)
                        P.I("dve", "scalar_tensor_tensor", out=tmpS[p][:, 0:nA, :], in0=bA[:, 0:nA * 128].rearrange("p (c q) -> p c q", q=128), scalar=sc_,
                            in1=bT[p][:, 0:nA, :], op0=ALU.mult, op1=ALU.add, reads=[tA_, tbT[p]], writes=[ttmpS[p]])
                        if nch == 5:
                            P.I("dve", "scalar_tensor_tensor", out=tmpS[p][:, 4, :], in0=bB[:, 0:128], scalar=sc_,
                                in1=bT[p][:, 4, :], op0=ALU.mult, op1=ALU.add, reads=[tB_, tbT[p]], writes=[ttmpS[p]])
                        P.I("act", "activation", out=PT[p][:, 0:nch, :], in_=tmpS[p][:, 0:nch, :], func=AF.Exp, reads=[ttmpS[p]], writes=[tPT[p]])
                    P.I("act", "activation", out=PT[p][:, 5:7, :], in_=bB[:, 128:384].rearrange("p (c q) -> p c q", q=128), func=AF.Exp, scale=sc_,
                        reads=[tB_], writes=[tPT[p]])
                    chunks = [(CTX // 128 + first + ci, ci) for ci in range(nch)] + [(0, 5), (1, 6)]
                    for j_, (tokc, pc) in enumerate(chunks):
                        P.I("pe", "matmul", bC[:, 0:128], lhsT=Va[:, tokc, h_ * 128:(h_ + 1) * 128], rhs=PT[p][:, pc, :],
                            start=(j_ == 0), stop=(j_ == len(chunks) - 1), reads=[tVa, tPT[p]], writes=[tC_])
                    for j_, (tokc, pc) in enumerate(chunks):
                        P.I("pe", "matmul", bC[:, 128:256], lhsT=ones_b[:], rhs=PT[p][:, pc, :],
                            start=(j_ == 0), stop=(j_ == len(chunks) - 1), reads=[tC, tPT[p]], writes=[tC_])
                    P.I("dve", "reciprocal", out=rden[:], in_=bC[:, 128:256], reads=[tC_], writes=[trden])
                    P.I("dve", "tensor_tensor", out=attnT[:, h_, :], in0=bC[:, 0:128], in1=rden[:], op=ALU.mult, reads=[tC_, trden], writes=[tattn])
                    P.I("act", "activation", out=sqA[:, h_, :], in_=attnT[:, h_, :], func=AF.Square, reads=[tattn], writes=[tsqA])
                for h_ in range(8):
                    P.I("pe", "matmul", PS[6][:, 0:128], lhsT=ones_b[:], rhs=sqA[:, h_, :], start=(h_ == 0), stop=(h_ == 7),
                        reads=[tC, tsqA], writes=[tPS[6]])
                rstd_bc_from_psum(6, 1024, rsA, trsA, 128)
                for h_ in range(8):
                    P.I("dve", "scalar_tensor_tensor", out=zA[:, h_, :], in0=attnT[:, h_, :], scalar=vec_fm["out_norm_fm"][:, l, h_:h_ + 1], in1=rsA[:],
                        op0=ALU.mult, op1=ALU.mult, reads=[tattn, trsA, tMOD], writes=[tzA])
                P.dma("sp", ZT[0:1024, qtok:qtok + 128].rearrange("(b p) t -> p b t", p=128), zA[:], reads=[tzA], writes=[tZT])
            P.flush()
        if cfg.stop_after == ("B", l):
            break
        with ExitStack() as es:
            S_ = lambda n, shp, dt: es.enter_context(nc.sbuf_tensor(f"{n}_{l}", shp, dt))
            TWO_PI = 2.0 * math.pi
            MAGIC = 12582912.0
            hpi = S_("hpi", [128, 1], F32)
            tS = T()
            P.I("dve", "memset", hpi[:], math.pi / 2.0, writes=[tS])

            lm_cache = {}
            cx_cache = []

            def lam_math(tag, shp, src_are, src_aim, src_ldt):
                if tag not in lm_cache:
                    lm_cache[tag] = [S_(f"lm_{tag}_{k}", shp, F32) for k in range(10)]
                lr, li, dt_, rho, th, kk, ph, ab, cs, sn = lm_cache[tag]
                P.dma("sp", lr[:], src_are, writes=[tS])
                P.dma("sp", li[:], src_aim, writes=[tS])
                P.dma("sp", dt_[:], src_ldt, writes=[tS])
                P.I("dve", "tensor_scalar_min", out=lr[:], in0=lr[:], scalar1=-1e-4, reads=[tS], writes=[tS])
                P.I("act", "activation", out=dt_[:], in_=dt_[:], func=AF.Exp, reads=[tS], writes=[tS])
                P.I("dve", "tensor_tensor", out=rho[:], in0=lr[:], in1=dt_[:], op=ALU.mult, reads=[tS], writes=[tS])
                P.I("act", "activation", out=rho[:], in_=rho[:], func=AF.Exp, reads=[tS], writes=[tS])
                P.I("dve", "tensor_tensor", out=th[:], in0=li[:], in1=dt_[:], op=ALU.mult, reads=[tS], writes=[tS])
                P.I("dve", "tensor_scalar", out=kk[:], in0=th[:], scalar1=1.0 / TWO_PI, scalar2=MAGIC, op0=ALU.mult, op1=ALU.add, reads=[tS], writes=[tS])
                P.I("dve", "tensor_scalar", out=kk[:], in0=kk[:], scalar1=-MAGIC, scalar2=-TWO_PI, op0=ALU.add, op1=ALU.mult, reads=[tS], writes=[tS])
                P.I("dve", "tensor_tensor", out=ph[:], in0=th[:], in1=kk[:], op=ALU.add, reads=[tS], writes=[tS])
                P.I("dve", "tensor_scalar", out=ph[:], in0=ph[:], scalar1=-math.pi, scalar2=math.pi, op0=ALU.max, op1=ALU.min, reads=[tS], writes=[tS])
                P.I("act", "activation", out=sn[:], in_=ph[:], func=AF.Sin, reads=[tS], writes=[tS])
                P.I("dve", "tensor_scalar", out=ab[:], in0=ph[:], scalar1=-1.0, scalar2=None, op0=ALU.mult, reads=[tS], writes=[tS])
                P.I("dve", "tensor_tensor", out=ab[:], in0=ab[:], in1=ph[:], op=ALU.max, reads=[tS], writes=[tS])
                P.I("act", "activation", out=cs[:], in_=ab[:], func=AF.Sin, scale=-1.0, bias=hpi[:, 0:1], reads=[tS], writes=[tS])
                return lr, li, rho, cs, sn

            Ec = S_("Ec", [128, 16, TS], F32)
            Es = S_("Es", [128, 16, TS], F32)
            et = [S_(f"et{k}", [128, 16, 128], F32) for k in range(4)]
            BTr = S_("BTr", [128, 4, 128], BF16)
            BTi = S_("BTi", [128, 4, 128], BF16)
            BT3r = S_("BT3r", [128, 4, 128], BF16)
            BT3i = S_("BT3i", [128, 4, 128], BF16)
            CTr = S_("CTr", [128, 16, 128], BF16)
            CTi = S_("CTi", [128, 16, 128], BF16)
            ctmp = S_("ctmp", [128, 16, 128], F32)
            carry = S_("carry", [128, 2, 16], F32)
            yTf = S_("yTf", [128, 4, TS], F32)
            NB = 5
            rtN = [[S_(f"rt{k}_{u}", [128, TS], F32) for k in range(4)] for u in range(NB)]
            trtN = [[T() for _ in range(4)] for u in range(NB)]
            gN = [[S_(f"g{k}_{u}", [128, TS], F32) for k in range(2)] for u in range(NB)]
            GN = [[S_(f"G{k}_{u}", [128, TS], F32) for k in range(2)] for u in range(NB)]
            hN = [[S_(f"h{k}_{u}", [128, TS], F32) for k in range(2)] for u in range(NB)]
            tgN, tGN, thN = [T() for _ in range(NB)], [T() for _ in range(NB)], [T() for _ in range(NB)]
            uTf2 = [S_("uTf_a", [128, 4, TS], F32), S_("uTf_b", [128, 4, TS], F32)]
            uTb2 = [S_("uTb_a", [128, 4, TS], BF16), S_("uTb_b", [128, 4, TS], BF16)]
            tuTf2, tuTb2 = [T(), T()], [T(), T()]
            hbr = [S_("hbr0", [128, TS], BF16), S_("hbr1", [128, TS], BF16)]
            hbi = [S_("hbi0", [128, TS], BF16), S_("hbi1", [128, TS], BF16)]
            thb = [T(), T()]
            tE, tBC, tcar0, tuTf, tuTb, tyTf = [T() for _ in range(6)]
            tcars = [T() for _ in range(16)]
            for d_ in range(2):
                lr, li, rho, cs, sn = lam_math("sm", [128, 16], din["s5_are_sm"][l, d_], din["s5_aim_sm"][l, d_], din["s5_ldt_sm"][l, d_])
                P.I("dve", "tensor_copy", out=Ec[:, :, 0], in_=cs[:], reads=[tS], writes=[tE])
                P.I("dve", "tensor_copy", out=Es[:, :, 0], in_=sn[:], reads=[tS], writes=[tE])
                m_ = 1
                while m_ < TS:
                    bc_c = Ec[:, :, m_ - 1:m_].to_broadcast([128, 16, m_])
                    bc_s = Es[:, :, m_ - 1:m_].to_broadcast([128, 16, m_])
                    P.I("dve", "tensor_tensor", out=et[0][:, :, 0:m_], in0=Ec[:, :, 0:m_], in1=bc_c, op=ALU.mult, reads=[tE], writes=[tE])
                    P.I("dve", "tensor_tensor", out=et[1][:, :, 0:m_], in0=Es[:, :, 0:m_], in1=bc_s, op=ALU.mult, reads=[tE], writes=[tE])
                    P.I("dve", "tensor_tensor", out=et[2][:, :, 0:m_], in0=Ec[:, :, 0:m_], in1=bc_s, op=ALU.mult, reads=[tE], writes=[tE])
                    P.I("dve", "tensor_tensor", out=et[3][:, :, 0:m_], in0=Es[:, :, 0:m_], in1=bc_c, op=ALU.mult, reads=[tE], writes=[tE])
                    P.I("dve", "tensor_tensor", out=Ec[:, :, m_:2 * m_], in0=et[0][:, :, 0:m_], in1=et[1][:, :, 0:m_], op=ALU.subtract, reads=[tE], writes=[tE])
                    P.I("dve", "tensor_tensor", out=Es[:, :, m_:2 * m_], in0=et[2][:, :, 0:m_], in1=et[3][:, :, 0:m_], op=ALU.add, reads=[tE], writes=[tE])
                    m_ *= 2
                lrx, lix, rhox, csx, snx = lam_math("ce", [128, 4, 128], din["s5_are_ce"][l, d_], din["s5_aim_ce"][l, d_], din["s5_ldt_ce"][l, d_])
                if not cx_cache:
                    cx_cache.extend([S_(f"cx_{k}", [128, 4, 128], F32) for k in range(9)])
                lbr, lbi, nr, ni, den, t1, t2, bre, bim = cx_cache
                I_ = lambda *a, **k: P.I(*a, reads=[tS], writes=[tS], **k)
                I_("dve", "tensor_tensor", out=lbr[:], in0=rhox[:], in1=csx[:], op=ALU.mult)
                I_("dve", "tensor_scalar_add", out=lbr[:], in0=lbr[:], scalar1=-1.0)
                I_("dve", "tensor_tensor", out=lbi[:], in0=rhox[:], in1=snx[:], op=ALU.mult)
                I_("dve", "tensor_tensor", out=t1[:], in0=lbr[:], in1=lrx[:], op=ALU.mult)
                I_("dve", "tensor_tensor", out=t2[:], in0=lbi[:], in1=lix[:], op=ALU.mult)
                I_("dve", "tensor_tensor", out=nr[:], in0=t1[:], in1=t2[:], op=ALU.add)
                I_("dve", "tensor_tensor", out=t1[:], in0=lbi[:], in1=lrx[:], op=ALU.mult)
                I_("dve", "tensor_tensor", out=t2[:], in0=lbr[:], in1=lix[:], op=ALU.mult)
                I_("dve", "tensor_tensor", out=ni[:], in0=t1[:], in1=t2[:], op=ALU.subtract)
                I_("dve", "tensor_tensor", out=t1[:], in0=lrx[:], in1=lrx[:], op=ALU.mult)
                I_("dve", "tensor_tensor", out=t2[:], in0=lix[:], in1=lix[:], op=ALU.mult)
                I_("dve", "tensor_tensor", out=den[:], in0=t1[:], in1=t2[:], op=ALU.add)
                I_("dve", "reciprocal", out=den[:], in_=den[:])
                I_("dve", "tensor_tensor", out=nr[:], in0=nr[:], in1=den[:], op=ALU.mult)
                I_("dve", "tensor_tensor", out=ni[:], in0=ni[:], in1=den[:], op=ALU.mult)
                P.dma("sp", bre[:], din["s5_bre_ce"][l, d_], writes=[tS])
                P.dma("sp", bim[:], din["s5_bim_ce"][l, d_], writes=[tS])
                I_("dve", "tensor_tensor", out=t1[:], in0=nr[:], in1=bre[:], op=ALU.mult)
                I_("dve", "tensor_tensor", out=t2[:], in0=ni[:], in1=bim[:], op=ALU.mult)
                P.I("dve", "tensor_tensor", out=BTr[:], in0=t1[:], in1=t2[:], op=ALU.subtract, reads=[tS], writes=[tBC])
                I_("dve", "tensor_tensor", out=t1[:], in0=nr[:], in1=bim[:], op=ALU.mult)
                I_("dve", "tensor_tensor", out=t2[:], in0=ni[:], in1=bre[:], op=ALU.mult)
                P.I("dve", "tensor_tensor", out=BTi[:], in0=t1[:], in1=t2[:], op=ALU.add, reads=[tS], writes=[tBC])
                P.I("dve", "tensor_scalar", out=BT3r[:], in0=BTr[:], scalar1=pmask[:, 3:4], scalar2=None, op0=ALU.mult, reads=[tBC, tC], writes=[tBC])
                P.I("dve", "tensor_scalar", out=BT3i[:], in0=BTi[:], scalar1=pmask[:, 3:4], scalar2=None, op0=ALU.mult, reads=[tBC, tC], writes=[tBC])
                P.dma("sp", ctmp[:], din["s5_cre"][l, d_], writes=[tS])
                P.I("act", "copy", out=CTr[:], in_=ctmp[:], reads=[tS], writes=[tBC])
                P.dma("sp", ctmp[:], din["s5_cim"][l, d_], reads=[tBC], writes=[tS])
                P.I("act", "mul", out=CTi[:], in_=ctmp[:], mul=-1.0, reads=[tS], writes=[tBC])
                P.I("dve", "memset", carry[:], 0.0, writes=tcars)
                units = [(ck, c_, q_) for ck in range(NST) for c_ in range(4) for q_ in range(4)]
                NU = len(units)

                def chunk_tok(ck):
                    if d_ == 0:
                        return ck * TS, False
                    return (0 if ck == 0 else CTX + L - ck * TS), True

                def ph_load(ck):
                    ta, rev = chunk_tok(ck)
                    cb = ck % 2
                    P.dma("sp", uTf2[cb][:], SINT[:, ta:ta + TS].rearrange("(c p) t -> p c t", p=128), reads=[tSINT], writes=[tuTf2[cb]])
                    src = uTf2[cb][:, :, ::-1] if rev else uTf2[cb][:]
                    P.I("act", "copy", out=uTb2[cb][:], in_=src, reads=[tuTf2[cb]], writes=[tuTb2[cb]])

                def ph0(u):
                    ck, c_, q_ = units[u]
                    cb, pp = ck % 2, u % 2
                    bR, bI, tR, tI = PS[2 * pp], PS[2 * pp + 1], tPS[2 * pp], tPS[2 * pp + 1]
                    if q_ < 3:
                        lr_, li_, rsl = BTr[32 * q_:32 * q_ + 32, c_, :], BTi[32 * q_:32 * q_ + 32, c_, :], slice(32 * q_, 32 * q_ + 32)
                    else:
                        lr_, li_, rsl = BT3r[64:128, c_, :], BT3i[64:128, c_, :], slice(64, 128)
                    P.I("pe", "matmul", bR[:, 0:TS], lhsT=lr_, rhs=uTb2[cb][rsl, c_, :], start=True, stop=True, reads=[tBC, tuTb2[cb]], writes=[tR])
                    P.I("pe", "matmul", bI[:, 0:TS], lhsT=li_, rhs=uTb2[cb][rsl, c_, :], start=True, stop=True, reads=[tBC, tuTb2[cb]], writes=[tI])

                def ph1(u):
                    ck, c_, q_ = units[u]
                    i_, pp, sl = c_ * 4 + q_, u % 2, u % NB
                    bR, bI, tR, tI = PS[2 * pp], PS[2 * pp + 1], tPS[2 * pp], tPS[2 * pp + 1]
                    rt_, trt = rtN[sl], trtN[sl]
                    P.I("dve", "tensor_tensor", out=rt_[0][:], in0=bR[:, 0:TS], in1=Ec[:, i_, :], op=ALU.mult, reads=[tR, tE], writes=[trt[0]])
                    P.I("dve", "tensor_tensor", out=rt_[1][:], in0=bI[:, 0:TS], in1=Es[:, i_, :], op=ALU.mult, reads=[tI, tE], writes=[trt[1]])
                    P.I("dve", "tensor_tensor", out=rt_[2][:], in0=bI[:, 0:TS], in1=Ec[:, i_, :], op=ALU.mult, reads=[tI, tE], writes=[trt[2]])
                    P.I("dve", "tensor_tensor", out=rt_[3][:], in0=bR[:, 0:TS], in1=Es[:, i_, :], op=ALU.mult, reads=[tR, tE], writes=[trt[3]])

                def ph2(u):
                    sl = u % NB
                    rt_, trt = rtN[sl], trtN[sl]
                    P.I("pool", "tensor_tensor", out=gN[sl][0][:], in0=rt_[0][:], in1=rt_[1][:], op=ALU.add, reads=[trt[0], trt[1]], writes=[tgN[sl]])
                    P.I("pool", "tensor_tensor", out=gN[sl][1][:], in0=rt_[2][:], in1=rt_[3][:], op=ALU.subtract, reads=[trt[2], trt[3]], writes=[tgN[sl]])

                def ph3(u):
                    ck, c_, q_ = units[u]
                    i_, sl = c_ * 4 + q_, u % NB
                    for ri in range(2):
                        P.I("dve", "tensor_tensor_scan", out=GN[sl][ri][:], data0=rho[:, i_:i_ + 1].to_broadcast([128, TS]), data1=gN[sl][ri][:],
                            initial=carry[:, ri, i_:i_ + 1], op0=ALU.mult, op1=ALU.add, reads=[tgN[sl], tS, tcars[i_]], writes=[tGN[sl]])

                def ph4(u):
                    ck, c_, q_ = units[u]
                    i_, sl = c_ * 4 + q_, u % NB
                    rt_, trt = rtN[sl], trtN[sl]
                    P.I("pool", "tensor_tensor", out=rt_[0][:], in0=GN[sl][0][:], in1=Ec[:, i_, :], op=ALU.mult, reads=[tGN[sl], tE], writes=[trt[0]])
                    P.I("pool", "tensor_tensor", out=rt_[1][:], in0=GN[sl][1][:], in1=Es[:, i_, :], op=ALU.mult, reads=[tGN[sl], tE], writes=[trt[1]])
                    P.I("pool", "tensor_tensor", out=rt_[2][:], in0=GN[sl][0][:], in1=Es[:, i_, :], op=ALU.mult, reads=[tGN[sl], tE], writes=[trt[2]])
                    P.I("pool", "tensor_tensor", out=rt_[3][:], in0=GN[sl][1][:], in1=Ec[:, i_, :], op=ALU.mult, reads=[tGN[sl], tE], writes=[trt[3]])

                def ph5(u):
                    sl = u % NB
                    rt_, trt = rtN[sl], trtN[sl]
                    P.I("dve", "tensor_tensor", out=hN[sl][0][:], in0=rt_[0][:], in1=rt_[1][:], op=ALU.subtract, reads=[trt[0], trt[1]], writes=[thN[sl]])
                    P.I("dve", "tensor_tensor", out=hN[sl][1][:], in0=rt_[2][:], in1=rt_[3][:], op=ALU.add, reads=[trt[2], trt[3]], writes=[thN[sl]])

                def ph6(u):
                    ck, c_, q_ = units[u]
                    i_, sl, pp = c_ * 4 + q_, u % NB, u % 2
                    ta, rev = chunk_tok(ck)
                    yb = 4 + (c_ % 2)
                    P.I("act", "copy", out=carry[:, 0, i_:i_ + 1], in_=hN[sl][0][:, TS - 1:TS], reads=[thN[sl]], writes=[tcars[i_]])
                    P.I("act", "copy", out=carry[:, 1, i_:i_ + 1], in_=hN[sl][1][:, TS - 1:TS], reads=[thN[sl]], writes=[tcars[i_]])
                    P.I("act", "copy", out=hbr[pp][:], in_=hN[sl][0][:], reads=[thN[sl]], writes=[thb[pp]])
                    P.I("act", "copy", out=hbi[pp][:], in_=hN[sl][1][:], reads=[thN[sl]], writes=[thb[pp]])
                    P.I("pe", "matmul", PS[yb][:, 0:TS], lhsT=CTr[:, i_, :], rhs=hbr[pp][:], start=(q_ == 0), stop=False, reads=[tBC, thb[pp]], writes=[tPS[yb]])
                    P.I("pe", "matmul", PS[yb][:, 0:TS], lhsT=CTi[:, i_, :], rhs=hbi[pp][:], start=False, stop=(q_ == 3), reads=[tBC, thb[pp]], writes=[tPS[yb]])
                    if q_ == 3:
                        dst = yTf[:, c_, ::-1] if rev else yTf[:, c_, :]
                        P.I("act", "copy", out=dst, in_=PS[yb][:, 0:TS], reads=[tPS[yb]], writes=[tyTf])
                        if c_ == 3:
                            P.dma("sp", YS[d_, :, ta:ta + TS].rearrange("(c p) t -> p c t", p=128), yTf[:], reads=[tyTf], writes=[tYS])

                for t_ in range(NU + 6):
                    if t_ < NU:
                        if units[t_][1] == 0 and units[t_][2] == 0:
                            ph_load(units[t_][0])
                        ph0(t_)
                        ph1(t_)
                    if 0 <= t_ - 4 < NU:
                        ph5(t_ - 4)
                    if 0 <= t_ - 2 < NU:
                        ph3(t_ - 2)
                    if 0 <= t_ - 3 < NU:
                        ph4(t_ - 3)
                    if 0 <= t_ - 1 < NU:
                        ph2(t_ - 1)
                    if 0 <= t_ - 5 < NU:
                        ph6(t_ - 5)
            P.flush()
        if cfg.stop_after == ("C1", l):
            break
        with ExitStack() as es:
            S_ = lambda n, shp, dt: es.enter_context(nc.sbuf_tensor(f"{n}_{l}", shp, dt))
            GWf = S_("GWf", [128, 4, 512], F32)
            GW = S_("GW", [128, 4, 512], BF16)
            y0 = S_("y0", [128, 4, TS], F32)
            y1 = S_("y1", [128, 4, TS], F32)
            uu = S_("uu", [128, 4, TS], F32)
            yg = S_("yg", [128, 4, TS], F32)
            ygb = S_("ygb", [128, 4, TS], BF16)
            sg = S_("sg", [128, 4, TS], F32)
            sq2 = S_("sq2", [128, 4, TS], BF16)
            rs2 = S_("rs2", [128, TS], F32)
            z2 = S_("z2", [128, 4, TS], BF16)
            tGW, ty0, ty1, tuu, tyg, tygb, tsg, tsq2, trs2, tz2 = [T() for _ in range(10)]
            P.dma("sp", GWf[:], din["glu_w"][l].rearrange("(k p) n -> p k n", p=128), writes=[tGW])
            P.I("dve", "tensor_copy", out=GW[:], in_=GWf[:], reads=[tGW], writes=[tGW])
            for st in range(NST):
                if l == dep - 1 and st == 0:
                    continue
                t0 = st * TS
                P.dma("sp", y0[:], YS[0, :, t0:t0 + TS].rearrange("(c p) t -> p c t", p=128), reads=[tYS], writes=[ty0])
                P.dma("sp", y1[:], YS[1, :, t0:t0 + TS].rearrange("(c p) t -> p c t", p=128), reads=[tYS], writes=[ty1])
                P.dma("act", uu[:], SINT[:, t0:t0 + TS].rearrange("(c p) t -> p c t", p=128), reads=[tSINT], writes=[tuu])
                P.I("pool", "tensor_tensor", out=y0[:], in0=y0[:], in1=y1[:], op=ALU.add, reads=[ty0, ty1], writes=[ty0])
                for c_ in range(4):
                    P.I("dve", "scalar_tensor_tensor", out=y0[:, c_, :], in0=uu[:, c_, :], scalar=vec_fm["s5_d_fm"][:, l, c_:c_ + 1], in1=y0[:, c_, :],
                        op0=ALU.mult, op1=ALU.add, reads=[tuu, ty0, tMOD], writes=[ty0])
                P.I("act", "activation", out=yg[:], in_=y0[:], func=AF.Gelu, reads=[ty0], writes=[tyg])
                P.I("dve", "tensor_copy", out=ygb[:], in_=yg[:], reads=[tyg], writes=[tygb])
                for co in range(4):
                    bank = co % 2
                    for ki in range(4):
                        P.I("pe", "matmul", PS[bank][:, 0:TS], lhsT=GW[:, ki, co * 128:(co + 1) * 128], rhs=ygb[:, ki, :], start=(ki == 0), stop=(ki == 3),
                            reads=[tGW, tygb], writes=[tPS[bank]])
                    P.I("act", "activation", out=sg[:, co, :], in_=PS[bank][:, 0:TS], func=AF.Sigmoid, bias=vec_fm["glu_b_fm"][:, l, co:co + 1],
                        reads=[tPS[bank], tMOD], writes=[tsg])
                P.I("dve", "tensor_tensor", out=sg[:], in0=sg[:], in1=yg[:], op=ALU.mult, reads=[tsg, tyg], writes=[tsg])
                P.I("act", "activation", out=sq2[:], in_=sg[:], func=AF.Square, reads=[tsg], writes=[tsq2])
                for c_ in range(4):
                    P.I("pe", "matmul", PS[2][:, 0:TS], lhsT=ones_b[:], rhs=sq2[:, c_, :], start=(c_ == 0), stop=(c_ == 3), reads=[tC, tsq2], writes=[tPS[2]])
                rstd_bc_from_psum(2, 512, rs2, trs2, TS)
                for c_ in range(4):
                    P.I("dve", "scalar_tensor_tensor", out=z2[:, c_, :], in0=sg[:, c_, :], scalar=vec_fm["out_norm_fm"][:, l, 12 + c_:13 + c_], in1=rs2[:],
                        op0=ALU.mult, op1=ALU.mult, reads=[tsg, trs2, tMOD], writes=[tz2])
                P.dma("sp", ZT[1536:2048, t0:t0 + TS].rearrange("(b p) t -> p b t", p=128), z2[:], reads=[tz2], writes=[tZT])
            P.flush()
        if cfg.stop_after == ("C2", l):
            break
        with ExitStack() as es:
            S_ = lambda n, shp, dt: es.enter_context(nc.sbuf_tensor(f"{n}_{l}", shp, dt))
            ZTs = S_("ZTs", [128, 16, TS], BF16)
            wD = [S_("wD0", [128, 16, 512], BF16), S_("wD1", [128, 16, 512], BF16)]
            xd = [S_("xd0", [128, D], F32), S_("xd1", [128, D], F32)]
            gbc = [S_("gbc0", [128, D], F32), S_("gbc1", [128, D], F32)]
            tmpD = [S_("tmpD0", [128, 512], F32), S_("tmpD1", [128, 512], F32)]
            tZs, tgb_ = T(), T()
            twD, txd, ttmpD = [T(), T()], [T(), T()], [T(), T()]
            for r_ in range(2):
                P.dma("sp", gbc[r_][:], GATE[l, 0, r_], reads=[tGATE], writes=[tgb_])
            wk = 0
            for st in range(NST):
                if l == dep - 1 and st == 0:
                    continue
                r = 1 if st == 0 else 0
                t0 = st * TS
                P.dma("sp", ZTs[:], ZT[:, t0:t0 + TS].rearrange("(k p) t -> p k t", p=128), reads=[tZT], writes=[tZs])
                for ti in range(2):
                    P.dma("sp", xd[ti][:], X[t0 + ti * 128:t0 + (ti + 1) * 128, :], reads=[tXs[st]], writes=[txd[ti]])
                for cg in range(4):
                    b = wk % 2
                    wk += 1
                    P.dma("act", wD[b][:], WOUT[l, cg].rearrange("p (k n) -> p k n", n=512), reads=[tW[("wout", l, cg)]], writes=[twD[b]])
                    for ti in range(2):
                        bank = (cg * 2 + ti) % 4
                        for kk in range(16):
                            P.I("pe", "matmul", PS[bank][:, :], lhsT=ZTs[:, kk, ti * 128:(ti + 1) * 128], rhs=wD[b][:, kk, :], start=(kk == 0), stop=(kk == 15),
                                reads=[tZs, twD[b]], writes=[tPS[bank]])
                        P.I("dve", "tensor_tensor", out=tmpD[ti][:], in0=PS[bank][:, :], in1=gbc[r][:, cg * 512:(cg + 1) * 512], op=ALU.mult,
                            reads=[tPS[bank], tgb_], writes=[ttmpD[ti]])
                        P.I("pool", "tensor_tensor", out=xd[ti][:, cg * 512:(cg + 1) * 512], in0=xd[ti][:, cg * 512:(cg + 1) * 512], in1=tmpD[ti][:], op=ALU.add,
                            reads=[ttmpD[ti], txd[ti]], writes=[txd[ti]])
                for ti in range(2):
                    P.dma("sp", X[t0 + ti * 128:t0 + (ti + 1) * 128, :], xd[ti][:], reads=[txd[ti]], writes=[tXs[st]])
            P.flush()
        if cfg.stop_after == ("D", l):
            break
        with ExitStack() as es:
            S_ = lambda n, shp, dt: es.enter_context(nc.sbuf_tensor(f"{n}_{l}", shp, dt))
            xe = S_("xe", [128, D], F32)
            hTE = S_("hTE", [128, 16, TS], BF16)
            xnE = S_("xnE", [128, D], BF16)
            ssE = S_("ssE", [128, 1], F32)
            rtE = S_("rtE", [128, 1], F32)
            wE = [S_("wE0", [128, 16, 256], BF16), S_("wE1", [128, 16, 256], BF16)]
            qT = S_("qT", [128, 16, TS], BF16)
            k1b = S_("k1b", [128, 8, 128], BF16)
            k2b = S_("k2b", [128, 8, 128], BF16)
            s1 = [S_("s1_0", [128, 8, 128], F32), S_("s1_1", [128, 8, 128], F32)]
            s2 = [S_("s2_0", [128, 8, 128], F32), S_("s2_1", [128, 8, 128], F32)]
            v1 = S_("v1", [128, 8, 16], F32)
            v2 = S_("v2", [128, 8, 16], F32)
            best = S_("best", [128, 8, 16], F32)
            tmpm = S_("tmpm", [128, 256], F32)
            ebt = S_("ebt", [128, 8, 16], F32)
            zs = S_("zs", [128, 8], F32)
            cth = [S_("cth0", [128, 8], F32), S_("cth1", [128, 8], F32)]
            cthr = [S_("cthr0", [128, 8], F32), S_("cthr1", [128, 8], F32)]
            gel = [S_("gel0", [128, 8, TS], BF16), S_("gel1", [128, 8, TS], BF16)]
            NGB = 2
            Gb = [S_(f"Gb{i_}", [128, 8, 8, 128], BF16) for i_ in range(NGB)]
            tgel = [T(), T()]
            tGbp = [[T() for _ in range(4)] for _ in range(NGB)]
            gbk = 0
            pending = None
            PrAll = S_("PrAll", [128, 2, 8, 128], F32)
            Pr = [PrAll[:, 0], PrAll[:, 1]]
            cand = PrAll[:].rearrange("p a h k -> p (a h k)").rearrange("p (h c) -> p h c", c=256)
            WT = S_("WT", [128, 128, TS], BF16)
            Vb = [S_("Vb0", [128, D], BF16), S_("Vb1", [128, D], BF16)]
            gE = S_("gE", [128, D], F32)
            tmpE = S_("tmpE", [128, 512], F32)
            txe, thTE, tqT, tk, tv, tcth, tWT, tgE, ttmpE = [T() for _ in range(9)]
            twE, ts1, ts2, tPr, tVb = [[T(), T()] for _ in range(5)]
            tmpN = (xnE, ssE, rtE, xnE, T())
            kf = cand[:, :, 0:128]
            P.dma("sp", kf, din["k1T"][l].rearrange("h d k -> d h k"), writes=[tv, tPr[0], tPr[1]])
            P.I("dve", "tensor_copy", out=k1b[:], in_=kf, reads=[tv], writes=[tk])
            P.dma("sp", kf, din["k2T"][l].rearrange("h d k -> d h k"), reads=[tk], writes=[tv])
            P.I("dve", "tensor_copy", out=k2b[:], in_=kf, reads=[tv], writes=[tk])
            wk = 0
            vk = 0
            gk = 0
            last_r = None
            def e_fixed(st):
                nonlocal wk
                r = 1 if st == 0 else 0
                t0 = st * TS
                for ti in range(2):
                    P.dma("sp", xe[:], X[t0 + ti * 128:t0 + (ti + 1) * 128, :], reads=[tXs[st]], writes=[txe])
                    norm_modT(xe, txe, lambda c, r=r: Affn[:, l, c, r:r + 1], lambda c, r=r: modfm[:, l, 3 * 16 + c, r:r + 1],
                              hTE, thTE, slice(ti * 128, (ti + 1) * 128), tmpN)
                for g in range(8):
                    b = wk % 2
                    wk += 1
                    P.dma("act", wE[b][:], WQ[l, g].rearrange("p (k n) -> p k n", n=256), reads=[tW[("wq", l, g)]], writes=[twE[b]])
                    for blk in range(2):
                        bank = blk % 2
                        for kk in range(16):
                            P.I("pe", "matmul", PS[bank][:, 0:TS], lhsT=wE[b][:, kk, blk * 128:(blk + 1) * 128], rhs=hTE[:, kk, :], start=(kk == 0), stop=(kk == 15),
                                reads=[twE[b], thTE], writes=[tPS[bank]])
                        P.I("act", "copy", out=qT[:, g * 2 + blk, :], in_=PS[bank][:, 0:TS], reads=[tPS[bank]], writes=[tqT])
                for ti in range(2):
                    tsl = slice(ti * 128, (ti + 1) * 128)
                    for h_ in range(8):
                        P.I("pe", "matmul", PS[4 + h_ // 4][:, (h_ % 4) * 128:(h_ % 4 + 1) * 128], lhsT=qT[:, 2 * h_, tsl], rhs=k1b[:, h_, :], start=True, stop=True,
                            reads=[tqT, tk], writes=[tPS[4 + h_ // 4]])
                        P.I("pe", "matmul", PS[6 + h_ // 4][:, (h_ % 4) * 128:(h_ % 4 + 1) * 128], lhsT=qT[:, 2 * h_ + 1, tsl], rhs=k2b[:, h_, :], start=True, stop=True,
                            reads=[tqT, tk], writes=[tPS[6 + h_ // 4]])
                    for hb in range(2):
                        P.I("act", "copy", out=s1[ti][:, hb * 4:(hb + 1) * 4, :], in_=PS[4 + hb][:, :].rearrange("p (h k) -> p h k", k=128), reads=[tPS[4 + hb]], writes=[ts1[ti]])
                        P.I("act", "copy", out=s2[ti][:, hb * 4:(hb + 1) * 4, :], in_=PS[6 + hb][:, :].rearrange("p (h k) -> p h k", k=128), reads=[tPS[6 + hb]], writes=[ts2[ti]])
                    for (sx, tsx, vx) in ((s1[ti], ts1[ti], v1), (s2[ti], ts2[ti], v2)):
                        for h_ in range(8):
                            P.I("dve", "max", out=vx[:, h_, 0:8], in_=sx[:, h_, :], reads=[tsx], writes=[tv])
                            P.I("dve", "match_replace", out=tmpm[:, 0:128], in_to_replace=vx[:, h_, 0:8], in_values=sx[:, h_, :], imm_value=-1e30, reads=[tsx, tv], writes=[tv])
                            P.I("dve", "max", out=vx[:, h_, 8:16], in_=tmpm[:, 0:128], reads=[tv], writes=[tv])
                    P.I("dve", "tensor_tensor", out=cand[:].rearrange("p h (a b) -> p h a b", b=16), in0=v1[:].unsqueeze(3).to_broadcast([128, 8, 16, 16]),
                        in1=v2[:].unsqueeze(2).to_broadcast([128, 8, 16, 16]), op=ALU.add, reads=[tv], writes=[tv, tPr[0], tPr[1]])
                    for h_ in range(8):
                        P.I("dve", "max", out=best[:, h_, 0:8], in_=cand[:, h_, :], reads=[tv], writes=[tv])
                        P.I("dve", "match_replace", out=tmpm[:, :], in_to_replace=best[:, h_, 0:8], in_values=cand[:, h_, :], imm_value=-1e30, reads=[tv], writes=[tv])
                        P.I("dve", "max", out=best[:, h_, 8:16], in_=tmpm[:, :], reads=[tv], writes=[tv])
                    th_bc16 = best[:, :, 15:16].to_broadcast([128, 8, 16])
                    th_bc128 = best[:, :, 15:16].to_broadcast([128, 8, 128])
                    P.I("dve", "tensor_tensor", out=ebt[:], in0=best[:], in1=th_bc16, op=ALU.subtract, reads=[tv], writes=[tv])
                    P.I("act", "activation", out=ebt[:], in_=ebt[:], func=AF.Exp, reads=[tv], writes=[tv])
                    P.I("dve", "tensor_reduce", out=zs[:], in_=ebt[:], axis=mybir.AxisListType.X, op=ALU.add, reads=[tv], writes=[tv])
                    P.I("dve", "reciprocal", out=cth[ti][:], in_=zs[:], reads=[tv], writes=[tcth])
                    P.I("dve", "tensor_scalar", out=cthr[ti][:], in0=cth[ti][:], scalar1=0.9995, scalar2=None, op0=ALU.mult, reads=[tcth], writes=[tcth])
                    P.I("dve", "tensor_tensor", out=s1[ti][:], in0=s1[ti][:], in1=th_bc128, op=ALU.subtract, reads=[ts1[ti], tv], writes=[ts1[ti]])
                    P.I("act", "activation", out=s1[ti][:], in_=s1[ti][:], func=AF.Exp, reads=[ts1[ti]], writes=[ts1[ti]])
                    P.I("dve", "tensor_tensor", out=s1[ti][:], in0=s1[ti][:], in1=cth[ti][:].unsqueeze(2).to_broadcast([128, 8, 128]), op=ALU.mult,
                        reads=[ts1[ti], tcth], writes=[ts1[ti]])
                    P.I("act", "activation", out=s2[ti][:], in_=s2[ti][:], func=AF.Exp, reads=[ts2[ti]], writes=[ts2[ti]])

            def e_phase1(st):
                nonlocal wk, gk, gbk, pending
                for g in range(16):
                    gp = g % 2
                    for ub in range(4):
                        b = wk % 2
                        wk += 1
                        P.dma("act", wE[b][:], UT[l, g * 4 + ub].rearrange("p (k n) -> p k n", n=256),
                              reads=[tW[("ut", l, g * 4 + ub)]], writes=[twE[b]])
                        for j in range(2):
                            blk = ub * 2 + j
                            bank = blk % 2
                            for kk in range(16):
                                P.I("pe", "matmul", PS[bank][:, 0:TS], lhsT=wE[b][:, kk, j * 128:(j + 1) * 128], rhs=hTE[:, kk, :], start=(kk == 0), stop=(kk == 15),
                                    reads=[thTE, twE[b]], writes=[tPS[bank]])
                            P.I("act", "activation", out=gel[gp][:, blk, :], in_=PS[bank][:, 0:TS], func=AF.Gelu, reads=[tPS[bank]], writes=[tgel[gp]])
                    for ti in range(2):
                        gb = gbk % NGB
                        gbk += 1
                        gbanks = (2, 3) if gb == 0 else (6, 7)
                        for h_ in range(8):
                            pb = gk % 2
                            gk += 1
                            pr_eng = "pool" if h_ % 2 == 0 else "dve"
                            P.I(pr_eng, "tensor_tensor", out=Pr[pb][:], in0=s1[ti][:, h_, 8 * g:8 * g + 8].unsqueeze(2).to_broadcast([128, 8, 128]),
                                in1=s2[ti][:, h_, :].unsqueeze(1).to_broadcast([128, 8, 128]), op=ALU.mult, reads=[ts1[ti], ts2[ti], tv], writes=[tPr[pb]])
                            P.I("dve", "scalar_tensor_tensor", out=Gb[gb][:, h_, :, :], in0=Pr[pb][:], scalar=cthr[ti][:, h_:h_ + 1], in1=Pr[pb][:], op0=ALU.is_ge, op1=ALU.mult,
                                reads=[tPr[pb], tcth], writes=[tGbp[gb][h_ // 2]])
                            if h_ % 2 == 1:
                                P.I("pool", "tensor_tensor", out=Gb[gb][:, h_ - 1, :, :], in0=Gb[gb][:, h_ - 1, :, :], in1=Gb[gb][:, h_, :, :], op=ALU.add,
                                    reads=[tGbp[gb][h_ // 2]], writes=[tGbp[gb][h_ // 2]])
                        if pending is not None:
                            pending()
                        for blk in range(8):
                            gbank = gbanks[blk // 4]
                            for hp in range(4):
                                P.I("pe", "matmul", PS[gbank][:, (blk % 4) * 128:(blk % 4 + 1) * 128], lhsT=Gb[gb][:, 2 * hp, blk, :], rhs=ident_b[:], start=(hp == 0), stop=(hp == 3),
                                    reads=[tGbp[gb][hp], tC], writes=[tPS[gbank]])

                        def mk_pending(g=g, gp=gp, ti=ti, gbanks=gbanks):
                            def run():
                                for half in range(2):
                                    gbank = gbanks[half]
                                    P.I("dve", "tensor_tensor", out=WT[:, g * 8 + half * 4:g * 8 + half * 4 + 4, ti * 128:(ti + 1) * 128],
                                        in0=PS[gbank][:, :].rearrange("p (b t) -> p b t", t=128), in1=gel[gp][:, half * 4:half * 4 + 4, ti * 128:(ti + 1) * 128], op=ALU.mult,
                                        reads=[tPS[gbank], tgel[gp]], writes=[tWT])
                            return run
                        pending = mk_pending()
                if pending is not None:
                    pending()
                    pending = None

            def e_phase2(st):
                nonlocal vk, last_r
                r = 1 if st == 0 else 0
                t0 = st * TS
                if r != last_r:
                    P.dma("sp", gE[:], GATE[l, 1, r], reads=[tGATE], writes=[tgE])
                    last_r = r
                for blk in range(128):
                    b = vk % 2
                    vk += 1
                    P.dma("sp", Vb[b][:], VT[l, blk * 128:(blk + 1) * 128, :], reads=[tW[("vt", l)]], writes=[tVb[b]])
                    for ti in range(2):
                        for cg in range(4):
                            bank = ti * 4 + cg
                            P.I("pe", "matmul", PS[bank][:, :], lhsT=WT[:, blk, ti * 128:(ti + 1) * 128], rhs=Vb[b][:, cg * 512:(cg + 1) * 512], start=(blk == 0), stop=(blk == 127),
                                reads=[tWT, tVb[b]], writes=[tPS[bank]])
                for ti in range(2):
                    P.dma("sp", xe[:], X[t0 + ti * 128:t0 + (ti + 1) * 128, :], reads=[tXs[st]], writes=[txe])
                    for cg in range(4):
                        bank = ti * 4 + cg
                        P.I("dve", "tensor_tensor", out=tmpE[:], in0=PS[bank][:, :], in1=gE[:, cg * 512:(cg + 1) * 512], op=ALU.mult, reads=[tPS[bank], tgE], writes=[ttmpE])
                        P.I("pool", "tensor_tensor", out=xe[:, cg * 512:(cg + 1) * 512], in0=xe[:, cg * 512:(cg + 1) * 512], in1=tmpE[:], op=ALU.add,
                            reads=[ttmpE, txe], writes=[txe])
                    P.dma("sp", X[t0 + ti * 128:t0 + (ti + 1) * 128, :], xe[:], reads=[txe], writes=[tXs[st]])

            st_list = [st for st in range(NST) if not (l == dep - 1 and st == 0)]
            e_fixed(st_list[0])
            for si, st in enumerate(st_list):
                e_phase1(st)
                if si + 1 < len(st_list):
                    e_fixed(st_list[si + 1])
                e_phase2(st)
            P.flush()
        if cfg.stop_after == ("E", l):
            break
    if cfg.stop_after is None:
        with ExitStack() as es:
            S_ = lambda n, shp, dt: es.enter_context(nc.sbuf_tensor(f"{n}_{l}", shp, dt))
            gfin = S_("gfin", [128, D], F32)
            xf = [S_("xf0", [128, D], F32), S_("xf1", [128, D], F32)]
            of = [S_("of0", [128, D], F32), S_("of1", [128, D], F32)]
            jf = S_("jf", [128, D], BF16)
            ssf = [S_("ssf0", [128, 1], F32), S_("ssf1", [128, 1], F32)]
            tgf, tjf = T(), T()
            txf, tof, tssf = [T(), T()], [T(), T()], [T(), T()]
            P.dma("sp", gfin[:], din["final_norm"].partition_broadcast(128), writes=[tgf])
            for i_ in range(L // 128):
                b = i_ % 2
                tok = CTX + i_ * 128
                P.dma("sp", xf[b][:], X[tok:tok + 128, :], reads=[tXs[tok // TS]], writes=[txf[b]])
                P.I("act", "activation", out=jf[:], in_=xf[b][:], func=AF.Square, accum_out=ssf[b][:, 0:1], reads=[txf[b]], writes=[tjf, tssf[b]])
                P.I("act", "activation", out=ssf[b][:], in_=ssf[b][:], func=AF.Sqrt, scale=1.0 / D, bias=eps_t[:, 0:1], reads=[tssf[b], tC], writes=[tssf[b]])
                P.I("dve", "reciprocal", out=ssf[b][:], in_=ssf[b][:], reads=[tssf[b]], writes=[tssf[b]])
                P.I("dve", "scalar_tensor_tensor", out=of[b][:], in0=xf[b][:], scalar=ssf[b][:, 0:1], in1=gfin[:], op0=ALU.mult, op1=ALU.mult,
                    reads=[txf[b], tssf[b], tgf], writes=[tof[b]])
                P.dma("act", out_d[i_ * 128:(i_ + 1) * 128, :], of[b][:], reads=[tof[b]], writes=[tOUT])
            P.flush()
    P.wait_all("act", [tOUT])
    P.wait_all("sp", tXs + [tQT, tKT, tVV, tSINT, tZT, tYS, tGATE, tOUT])
    P.wait_all("pool", list(tW.values()))
    P.emit()
    return nc


def kernel(**inputs):
    cfg = Cfg(L=4096, depth=4)
    inp = {k: np.asarray(v) for k, v in inputs.items()}
    sh, info = prep_shared(inp, cfg)
    B = inp["x"].shape[0]
    in_maps = []
    for core in range(8):
        b = core % B
        m = dict(sh)
        m["x_in"] = np.ascontiguousarray(np.concatenate([inp["ctx"][b], inp["x"][b]], 0), dtype=np.float32)
        m["cc"] = np.ascontiguousarray(np.stack([inp["c"][b], inp["c_ctx"]], 0), dtype=np.float32)
        in_maps.append(m)
    shapes = {k: v.shape for k, v in in_maps[0].items()}
    nc = build(cfg, info, shapes)
    res = run_bass_kernel_spmd(nc, in_maps, core_ids=list(range(8)))
    out = np.stack([np.asarray(res.results[b]["out"], dtype=np.float32) for b in range(B)], 0)
    return out
```

```python
import math
from contextlib import ExitStack
import numpy as np
import concourse.bass as bass
import concourse.mybir as mybir
from concourse.bass_utils import run_bass_kernel_spmd

F32 = mybir.dt.float32
BF16 = mybir.dt.bfloat16
AF = mybir.ActivationFunctionType
ALU = mybir.AluOpType

SEM_LIMIT = 30000
N_DMA_SLOTS = 12


class T:
    __slots__ = ("w", "r")

    def __init__(self):
        self.w = None
        self.r = []


class Eng:
    def __init__(self, prog, name, handle):
        self.prog = prog
        self.name = name
        self.h = handle
        self.ops = []
        self.sems = []
        self.count = 0
        self.waited = {}
        self.dma_k = 0
        self.dma_slots = None

    def cur_token(self):
        k = self.count
        si = k // SEM_LIMIT
        while len(self.sems) <= si:
            self.sems.append(self.prog.nc.alloc_semaphore(f"s_{self.name}_{len(self.sems)}"))
        return (self.sems[si], k % SEM_LIMIT + 1, self.name)


class Prog:
    def __init__(self, nc):
        self.nc = nc
        self.E = {
            "pe": Eng(self, "pe", nc.tensor),
            "dve": Eng(self, "dve", nc.vector),
            "act": Eng(self, "act", nc.scalar),
            "pool": Eng(self, "pool", nc.gpsimd),
            "sp": Eng(self, "sp", nc.sync),
        }
        self.n_ops = 0
        self._tr = {}

    def sb(self, name, shape, dt):
        t = self.nc.alloc_sbuf_tensor(name, list(shape), dt)
        return t

    def _waits(self, eng, reads, writes, same_engine_sync):
        need = {}

        def add(tok):
            if tok is None:
                return
            sem, val, en = tok
            if en == eng.name and not same_engine_sync:
                return
            key = id(sem)
            if key not in need or need[key][1] < val:
                need[key] = (sem, val)

        for t in reads:
            add(t.w)
        for t in writes:
            add(t.w)
            for r in t.r:
                if r[2] != eng.name:
                    add(r)
        out = []
        for key, (sem, val) in need.items():
            if eng.waited.get(key, 0) >= val:
                continue
            eng.waited[key] = val
            out.append((sem, val))
        return out

    def _commit(self, tok, reads, writes):
        for t in reads:
            t.r.append(tok)
            if len(t.r) > 64:
                last = {}
                for r in t.r:
                    last[(id(r[0]))] = r if (id(r[0]) not in last or last[id(r[0])][1] < r[1]) else last[id(r[0])]
                t.r = list(last.values())
        for t in writes:
            t.w = tok
            t.r = []

    def op(self, en, fn, reads=(), writes=()):
        eng = self.E[en]
        sync_same = en in ("dve", "act", "pool")
        waits = self._waits(eng, reads, writes, sync_same)
        tok = eng.cur_token()
        eng.count += 1
        eng.ops.append((waits, fn, (tok[0], 1)))
        self._commit(tok, reads, writes)
        self.n_ops += 1
        return tok

    def I(self, en, mname, *args, reads=(), writes=(), **kw):
        def fn(h, mname=mname, args=args, kw=kw):
            return getattr(h, mname)(*args, **kw)
        return self.op(en, fn, reads, writes)

    def dma(self, q, out, in_, reads=(), writes=(), **kw):
        eng = self.E[q]
        if eng.dma_slots is None:
            eng.dma_slots = [[self.nc.alloc_semaphore(f"d_{q}_{i}"), 0] for i in range(N_DMA_SLOTS)]
        slot = eng.dma_slots[eng.dma_k % N_DMA_SLOTS]
        eng.dma_k += 1
        waits = self._waits(eng, reads, writes, True)
        sem = slot[0]
        if slot[1] > 0:
            key = id(sem)
            if eng.waited.get(key, 0) < slot[1]:
                eng.waited[key] = slot[1]
                waits.append((sem, slot[1]))
        slot[1] += 16
        tok = (sem, slot[1], "dma_" + q)

        def fn(h, out=out, in_=in_, kw=kw):
            return h.dma_start(out=out, in_=in_, **kw)
        eng.ops.append((waits, fn, (sem, 16)))
        self._commit(tok, reads, writes)
        self.n_ops += 1
        return tok

    def wait_all(self, en, trackers):
        eng = self.E[en]
        waits = self._waits(eng, trackers, (), True)
        eng.ops.append((waits, None, None))

    def barrier(self):
        toks = []
        for en, eng in self.E.items():
            if eng.count > 0:
                k = eng.count - 1
                toks.append((eng.sems[k // SEM_LIMIT], k % SEM_LIMIT + 1))
            if eng.dma_slots is not None and en != "pool":
                for sem, val in eng.dma_slots:
                    if val > 0:
                        toks.append((sem, val))
        for en, eng in self.E.items():
            waits = []
            for sem, val in toks:
                key = id(sem)
                if eng.waited.get(key, 0) >= val:
                    continue
                eng.waited[key] = val
                waits.append((sem, val))
            if waits:
                eng.ops.append((waits, None, None))

    def emit(self):
        nc = self.nc
        with nc.Block() as block:
            def mk(eng):
                ops = eng.ops
                eng.ops = []

                def body(h):
                    for waits, fn, inc in ops:
                        for sem, val in waits:
                            h.wait_ge(sem, val)
                        if fn is not None:
                            inst = fn(h)
                            inst.then_inc(inc[0], inc[1])
                return body
            if self.E["sp"].ops:
                block.sync(mk(self.E["sp"]))
            if self.E["pe"].ops:
                block.tensor(mk(self.E["pe"]))
            if self.E["dve"].ops:
                block.vector(mk(self.E["dve"]))
            if self.E["act"].ops:
                block.scalar(mk(self.E["act"]))
            if self.E["pool"].ops:
                block.gpsimd(mk(self.E["pool"]))

    def flush(self):
        self.barrier()
        self.emit()


D = 2048
CTX = 256
GRID_W = 64
NA_ROWS = 8
NA_COLS = 16
IN_W = 4608
NEXP = 16384
EPS = 1e-6
NEG = -30000.0
TS = 256


class Cfg:
    def __init__(self, L=4096, depth=4, debug=False, stop_after=None):
        self.L = L
        self.depth = depth
        self.T = L + CTX
        self.NST = self.T // TS
        self.rows = L // GRID_W
        self.debug = debug
        self.stop_after = stop_after


def na_patterns(rows):
    kr = min(NA_ROWS, rows)
    col = np.arange(GRID_W)
    col_start = np.clip(col - NA_COLS // 2, 0, GRID_W - NA_COLS)
    pats = {}
    plist = []
    info = []
    for qt in range(rows // 2):
        r0 = 2 * qt
        rs = [int(np.clip(r - kr // 2, 0, rows - kr)) for r in (r0, r0 + 1)]
        first = rs[0] // 2
        last = (rs[1] + kr - 1) // 2
        nch = last - first + 1
        assert nch <= 5
        idx = -np.ones((5, 128, 128), np.int64)
        for ci in range(nch):
            for kk in range(2):
                krow = 2 * (first + ci) + kk
                for qq in range(2):
                    r = r0 + qq
                    if not (rs[qq] <= krow < rs[qq] + kr):
                        continue
                    ro = krow - r + (NA_ROWS - 1)
                    for c in range(GRID_W):
                        cs = col_start[c]
                        kc = np.arange(cs, cs + NA_COLS)
                        co = kc - c + (NA_COLS - 1)
                        idx[ci, kk * 64 + kc, qq * 64 + c] = ro * (2 * NA_COLS - 1) + co
        key = idx.tobytes()
        if key not in pats:
            pats[key] = len(plist)
            plist.append(idx)
        info.append((first, nch, pats[key]))
    return info, np.stack(plist)


def prep_shared(inp, cfg):
    dep = cfg.depth
    f = lambda a: np.ascontiguousarray(a, dtype=np.float32)
    sh = {}
    sh["ada_w"] = f(inp["ada_w"][:dep])
    sh["ada_b"] = f(inp["ada_b"][:dep])
    sh["w_in"] = f(inp["w_in"][:dep])
    sh["w_out"] = f(inp["w_out"][:dep])
    sh["peer_wq"] = f(inp["peer_wq"][:dep])
    sh["peer_uT"] = f(np.transpose(inp["peer_u"][:dep], (0, 2, 1)))
    sh["peer_v"] = f(inp["peer_v"][:dep])
    sh["k1T"] = f(np.transpose(inp["peer_k1"][:dep], (0, 1, 3, 2)))
    sh["k2T"] = f(np.transpose(inp["peer_k2"][:dep], (0, 1, 3, 2)))
    sh["sgu_wT"] = f(np.transpose(inp["sgu_w"][:dep], (0, 1, 3, 2)))
    sh["sgu_b"] = f(inp["sgu_b"][:dep])
    sh["sgu_norm"] = f(inp["sgu_norm"][:dep])
    sh["glu_w"] = f(inp["s5_glu_w"][:dep])
    fm = lambda v: f(v.reshape(v.shape[0], -1, 128).transpose(0, 2, 1))
    sh["norm_mix_fm"] = fm(inp["norm_mix"][:dep])
    sh["norm_ffn_fm"] = fm(inp["norm_ffn"][:dep])
    sh["out_norm_fm"] = fm(inp["out_norm"][:dep])
    sh["s5_d_fm"] = fm(inp["s5_d"][:dep])
    sh["glu_b_fm"] = fm(inp["s5_glu_b"][:dep])
    sh["ada_b_fm"] = fm(inp["ada_b"][:dep])
    sh["final_norm"] = f(inp["final_norm"])
    info, pidx = na_patterns(cfg.rows)
    rpb = inp["na_rpb"][:dep].reshape(dep, 8, -1)
    tab = np.where(pidx[None, None] >= 0, rpb[:, :, np.clip(pidx, 0, None)], np.float32(NEG))
    sh["na_bias"] = f(tab)
    G, Pn, H = 32, 64, 16
    a_re = inp["s5_a_re"][:dep].reshape(dep, 2, 2048)
    a_im = inp["s5_a_im"][:dep].reshape(dep, 2, 2048)
    ldt = np.repeat(inp["s5_log_dt"][:dep], Pn, axis=-1)
    sm = lambda v: f(v.reshape(dep, 2, 16, 128).transpose(0, 1, 3, 2))
    sh["s5_are_sm"], sh["s5_aim_sm"], sh["s5_ldt_sm"] = sm(a_re), sm(a_im), sm(ldt)
    def ce(v):
        vt = v.reshape(dep, 2, 16, 128)
        o = np.repeat(vt, 32, axis=2)
        return f(o.reshape(dep, 2, 4, 128, 128).transpose(0, 1, 3, 2, 4))
    sh["s5_are_ce"], sh["s5_aim_ce"], sh["s5_ldt_ce"] = ce(a_re), ce(a_im), ce(ldt)

    def bT(b):
        o = np.zeros((dep, 2, 512, 128), np.float32)
        for g in range(G):
            for h in range(H):
                ch = g * 16 + h
                o[:, :, ch, (g % 2) * 64:(g % 2) * 64 + 64] = b[:, :, g, :, h]
        return f(o.reshape(dep, 2, 4, 128, 128).transpose(0, 1, 3, 2, 4))
    sh["s5_bre_ce"] = bT(inp["s5_b_re"][:dep])
    sh["s5_bim_ce"] = bT(inp["s5_b_im"][:dep])

    def cT(c):
        o = np.zeros((dep, 2, 128, 16, 128), np.float32)
        for g in range(G):
            i = g // 2
            q = i % 4
            for h in range(H):
                chl = 32 * q + (g % 2) * 16 + h
                o[:, :, (g % 2) * 64:(g % 2) * 64 + 64, i, chl] = c[:, :, g, h, :]
        return f(o)
    sh["s5_cre"] = cT(inp["s5_c_re"][:dep])
    sh["s5_cim"] = cT(inp["s5_c_im"][:dep])
    sh["ident"] = np.eye(128, dtype=np.float32)
    sh["pmask"] = f((np.arange(128)[:, None] // 32 == np.arange(4)[None, :]).astype(np.float32))
    return sh, info


def build(cfg, na_info, shapes):
    nc = bass.Bass("TRN2", target_bir_lowering=False)
    P = Prog(nc)
    dep, Tn, L, NST = cfg.depth, cfg.T, cfg.L, cfg.NST
    din = {}
    for name, shp in shapes.items():
        din[name] = nc.dram_tensor(name, list(shp), F32, kind="ExternalInput").ap()
    dbg = cfg.debug
    skind = "ExternalOutput" if dbg else "Internal"
    out_d = nc.dram_tensor("out", [L, D], F32, kind="ExternalOutput").ap()

    def scratch(name, shape, dt, always_internal=False):
        k = "Internal" if always_internal else skind
        return nc.dram_tensor(name, list(shape), dt, kind=k).ap()

    X = scratch("X", [Tn, D], F32)
    QT = scratch("QT", [1024, Tn], BF16)
    KT = scratch("KT", [1024, Tn], BF16)
    VV = scratch("VV", [Tn, 1024], BF16)
    SINT = scratch("SINT", [512, Tn], F32)
    ZT = scratch("ZT", [D, Tn], BF16)
    YS = scratch("YS", [2, 512, Tn], F32)
    GATE = scratch("GATE", [dep, 2, 2, 128, D], F32, always_internal=True)
    WIN = scratch("WINb", [dep, 9, 128, 16 * 512], BF16, True)
    WOUT = scratch("WOUTb", [dep, 4, 128, 16 * 512], BF16, True)
    WQ = scratch("WQb", [dep, 8, 128, 16 * 256], BF16, True)
    UT = scratch("UTb", [dep, 64, 128, 16 * 256], BF16, True)
    VT = scratch("VTb", [dep, NEXP, D], BF16, True)
    tX, tQT, tKT, tVV, tSINT, tZT, tYS, tGATE, tOUT = [T() for _ in range(9)]
    tXs = [T() for _ in range(NST)]
    tW = {}

    ident_f = P.sb("ident_f", [128, 128], F32)
    ident_b = P.sb("ident_b", [128, 128], BF16)
    ones_b = P.sb("ones_b", [128, 128], BF16)
    eps_t = P.sb("eps_t", [128, 1], F32)
    tC = T()
    P.dma("sp", ident_f[:], din["ident"], writes=[tC])
    P.op("dve", lambda h: h.tensor_copy(out=ident_b[:], in_=ident_f[:]), reads=[tC], writes=[tC])
    P.op("dve", lambda h: h.memset(ones_b[:], 1.0), writes=[tC])
    P.op("dve", lambda h: h.memset(eps_t[:], EPS), writes=[tC])
    pmask = P.sb("pmask_sb", [128, 4], F32)
    P.dma("sp", pmask[:], din["pmask"], writes=[tC])

    PS = [nc.alloc_psum_tensor(f"ps{i}", [128, 512], F32) for i in range(8)]
    tPS = [T() for _ in range(8)]

    def cast_weights(l):
        for name, src, dst, ng, w in (("win", din["w_in"], WIN, 9, 512), ("wout", din["w_out"], WOUT, 4, 512),
                                     ("wq", din["peer_wq"], WQ, 8, 256), ("ut", din["peer_uT"], UT, 64, 256)):
            for g in range(ng):
                t = T()
                tW[(name, l, g)] = t
                P.dma("pool", dst[l, g].rearrange("p (k n) -> p k n", n=w), src[l, :, g * w:(g + 1) * w].rearrange("(k p) n -> p k n", p=128), writes=[t])
        t = T()
        tW[("vt", l)] = t
        for r0 in range(0, NEXP, 256):
            P.dma("pool", VT[l, r0:r0 + 256, :], din["peer_v"][l, r0:r0 + 256, :], writes=[t])

    for r0 in range(0, Tn, 256):
        P.dma("sp", X[r0:r0 + 256, :], din["x_in"][r0:r0 + 256, :], writes=[tXs[r0 // 256]])

    modfm = P.sb("modfm", [128, dep, 96, 2], F32)
    Amix = P.sb("Amix", [128, dep, 16, 2], F32)
    Affn = P.sb("Affn", [128, dep, 16, 2], F32)
    tMOD = T()
    scT = P.sb("scT", [128, 16, 2], F32)
    tSC = T()
    for r_ in range(2):
        P.dma("sp", scT[:, :, r_], din["cc"][r_].rearrange("(k p) -> p k", p=128), writes=[tSC], allow_slow_non_contiguous=True)
    sig = P.sb("sig_sc", [128, 16, 2], F32)
    P.op("act", lambda h: h.activation(out=sig[:], in_=scT[:], func=AF.Silu), reads=[tSC], writes=[tSC])
    scS = sig
    vec_fm = {}
    for nm in ("norm_mix_fm", "norm_ffn_fm", "out_norm_fm", "ada_b_fm", "s5_d_fm", "glu_b_fm"):
        shp = shapes[nm]
        tt = P.sb("v_" + nm, [128, dep, shp[2]], F32)
        P.dma("sp", tt[:], din[nm].rearrange("l p k -> p l k"), writes=[tMOD])
        vec_fm[nm] = tt

    with ExitStack() as es:
        aw0 = es.enter_context(nc.sbuf_tensor("adaw_a", [128, 16, 128], F32))
        aw1 = es.enter_context(nc.sbuf_tensor("adaw_b", [128, 16, 128], F32))
        ag0 = es.enter_context(nc.sbuf_tensor("adag_a", [128, 16, 512], F32))
        ag1 = es.enter_context(nc.sbuf_tensor("adag_b", [128, 16, 512], F32))
        gtmp = es.enter_context(nc.sbuf_tensor("gtmp", [128, 512], F32))
        gbias = es.enter_context(nc.sbuf_tensor("gbias", [128, 512], F32))
        aws = [aw0, aw1]
        taw = [T(), T()]
        ags = [ag0, ag1]
        tag = [T(), T()]
        tgt, tgb = T(), T()
        k = 0
        for l in range(dep):
            for j in range(96):
                b = k % 2
                k += 1
                P.dma("sp", aws[b][:], din["ada_w"][l, :, j * 128:(j + 1) * 128].rearrange("(k p) n -> p k n", p=128),
                      writes=[taw[b]])
                bank = k % 2
                for kk in range(16):
                    P.op("pe", lambda h, b=b, kk=kk, bank=bank: h.matmul(PS[bank][:, 0:2], lhsT=aws[b][:, kk, :], rhs=scS[:, kk, :],
                                                                       start=(kk == 0), stop=(kk == 15)),
                         reads=[taw[b], tSC], writes=[tPS[bank]])
                P.op("dve", lambda h, l=l, j=j, bank=bank: h.tensor_scalar(out=modfm[:, l, j, :], in0=PS[bank][:, 0:2],
                                                                          scalar1=vec_fm["ada_b_fm"][:, l, j:j + 1], scalar2=None,
                                                                          op0=ALU.add),
                     reads=[tPS[bank], tMOD], writes=[tMOD])
            for (Aout, gname, sidx) in ((Amix, "norm_mix_fm", 1), (Affn, "norm_ffn_fm", 4)):
                for r in range(2):
                    P.op("dve", lambda h, Aout=Aout, gname=gname, sidx=sidx, r=r, l=l: h.scalar_tensor_tensor(
                        out=Aout[:, l, :, r], in0=modfm[:, l, sidx * 16:(sidx + 1) * 16, r], scalar=1.0,
                        in1=vec_fm[gname][:, l, :], op0=ALU.add, op1=ALU.mult), reads=[tMOD], writes=[tMOD])
            kg = 0
            for which, gidx in ((0, 2), (1, 5)):
                for cg in range(4):
                    b = kg % 2
                    kg += 1
                    c0 = gidx * D + cg * 512
                    P.dma("sp", ags[b][:], din["ada_w"][l, :, c0:c0 + 512].rearrange("(k p) n -> p k n", p=128), writes=[tag[b]])
                    P.dma("sp", gbias[:], din["ada_b"][l, c0:c0 + 512].partition_broadcast(128), writes=[tgb])
                    for r in range(2):
                        bank = 2 + (r % 2)
                        for kk in range(16):
                            P.op("pe", lambda h, b=b, kk=kk, r=r, bank=bank: h.matmul(
                                PS[bank][:, :], lhsT=scS[:, kk, r:r + 1].to_broadcast([128, 128]), rhs=ags[b][:, kk, :],
                                start=(kk == 0), stop=(kk == 15)), reads=[tag[b], tSC], writes=[tPS[bank]])
                        P.op("dve", lambda h, bank=bank: h.tensor_tensor(out=gtmp[:], in0=PS[bank][:, :], in1=gbias[:], op=ALU.add),
                             reads=[tPS[bank], tgb], writes=[tgt])
                        P.dma("sp", GATE[l, which, r, :, cg * 512:(cg + 1) * 512], gtmp[:], reads=[tgt], writes=[tGATE])
        P.flush()

    for l in range(dep):
        cast_weights(l)

    def norm_modT(xt, txt, A_ap, B_ap, hT, thT, tsl, tmp):
        junk, ss, rt, xn, tj = tmp
        P.op("act", lambda h: h.activation(out=junk[:], in_=xt[:], func=AF.Square, accum_out=ss[:, 0:1]), reads=[txt], writes=[tj])
        P.op("act", lambda h: h.activation(out=rt[:], in_=ss[:], func=AF.Sqrt, scale=1.0 / D, bias=eps_t[:, 0:1]), reads=[tj, tC], writes=[tj])
        P.op("dve", lambda h: h.reciprocal(out=rt[:], in_=rt[:]), reads=[tj], writes=[tj])
        P.op("act", lambda h: h.activation(out=xn[:], in_=xt[:], func=AF.Copy, scale=rt[:, 0:1]), reads=[txt, tj], writes=[tj])
        for c4 in range(4):
            bank = 4 + (c4 % 2)
            psb = PS[bank][:].bitcast(BF16)
            for j in range(4):
                c = c4 * 4 + j
                P.op("pe", lambda h, c=c, j=j, psb=psb: h.transpose(out=psb[:, j * 128:(j + 1) * 128], in_=xn[:, c * 128:(c + 1) * 128],
                                                                    identity=ident_b[:]), reads=[tj, tC], writes=[tPS[bank]])
            for j in range(4):
                c = c4 * 4 + j
                eng = "dve" if j % 2 == 0 else "act"
                if eng == "dve":
                    P.I("dve", "tensor_scalar", out=hT[:, c, tsl], in0=psb[:, j * 128:(j + 1) * 128],
                        scalar1=A_ap(c), scalar2=B_ap(c), op0=ALU.mult, op1=ALU.add,
                        reads=[tPS[bank], tMOD], writes=[thT])
                else:
                    P.I("act", "activation", out=hT[:, c, tsl], in_=psb[:, j * 128:(j + 1) * 128],
                        func=AF.Identity, scale=A_ap(c), bias=B_ap(c),
                        reads=[tPS[bank], tMOD], writes=[thT])

    def rstd_bc_from_psum(bank, nfeat, dst, tdst, width):
        P.op("act", lambda h: h.activation(out=dst[:, 0:width], in_=PS[bank][:, 0:width], func=AF.Sqrt, scale=1.0 / nfeat, bias=eps_t[:, 0:1]),
             reads=[tPS[bank], tC], writes=[tdst])
        P.op("dve", lambda h: h.reciprocal(out=dst[:, 0:width], in_=dst[:, 0:width]), reads=[tdst], writes=[tdst])

    for l in range(dep):
        with ExitStack() as es:
            xa0 = es.enter_context(nc.sbuf_tensor(f"xa0_{l}", [128, D], F32))
            xa1 = es.enter_context(nc.sbuf_tensor(f"xa1_{l}", [128, D], F32))
            ss = es.enter_context(nc.sbuf_tensor(f"ssA_{l}", [128, 1], F32))
            rt = es.enter_context(nc.sbuf_tensor(f"rtA_{l}", [128, 1], F32))
            xn = es.enter_context(nc.sbuf_tensor(f"xnA_{l}", [128, D], BF16))
            hT = es.enter_context(nc.sbuf_tensor(f"hTA_{l}", [128, 16, TS], BF16))
            wA0 = es.enter_context(nc.sbuf_tensor(f"wA0_{l}", [128, 16, 512], BF16))
            wA1 = es.enter_context(nc.sbuf_tensor(f"wA1_{l}", [128, 16, 512], BF16))
            wA2 = es.enter_context(nc.sbuf_tensor(f"wA2_{l}", [128, 16, 512], BF16))
            evb = es.enter_context(nc.sbuf_tensor(f"evb_{l}", [128, 4, TS], BF16))
            evf = es.enter_context(nc.sbuf_tensor(f"evf_{l}", [128, 4, TS], F32))
            uT = es.enter_context(nc.sbuf_tensor(f"uT_{l}", [128, 4, TS], F32))
            vtok = es.enter_context(nc.sbuf_tensor(f"vtok_{l}", [128, 2, 1024], BF16))
            svg = es.enter_context(nc.sbuf_tensor(f"svg_{l}", [128, 512], F32))
            svq = es.enter_context(nc.sbuf_tensor(f"svq_{l}", [128, 512], F32))
            vsg = es.enter_context(nc.sbuf_tensor(f"vsg_{l}", [128, 2, 512], BF16))
            gsgu = es.enter_context(nc.sbuf_tensor(f"gsgu_{l}", [128, 512], F32))
            bsb = es.enter_context(nc.sbuf_tensor(f"bsb_{l}", [128, 4, 128], F32))
            wsT = es.enter_context(nc.sbuf_tensor(f"wsT_{l}", [128, 4, 128], BF16))
            wsTf = es.enter_context(nc.sbuf_tensor(f"wsTf_{l}", [128, 4, 128], F32))
            sguT = es.enter_context(nc.sbuf_tensor(f"sguT_{l}", [128, 4, 128], F32))
            sqb = es.enter_context(nc.sbuf_tensor(f"sqb_{l}", [128, 4, 128], BF16))
            rsb = es.enter_context(nc.sbuf_tensor(f"rsb_{l}", [128, 128], F32))
            zsg = es.enter_context(nc.sbuf_tensor(f"zsg_{l}", [128, 4, 128], BF16))
            xa = [xa0, xa1]
            txa = [T(), T()]
            thT = T()
            wA = [wA0, wA1, wA2]
            twA = [(T(), T()) for _ in range(3)]
            tev, tevf, tuT, tvtok, tsv, tvsg, tsguc, tsguT, tsq, trs, tzs = [T() for _ in range(11)]
            tmp = (xn, ss, rt, xn, T())
            P.dma("sp", gsgu[:], din["sgu_norm"][l].partition_broadcast(128), writes=[tsguc])
            P.dma("sp", bsb[:], din["sgu_b"][l].partition_broadcast(128), writes=[tsguc])
            P.dma("sp", wsTf[:], din["sgu_wT"][l].rearrange("h s t -> s h t"), writes=[tsguc])
            P.op("dve", lambda h: h.tensor_copy(out=wsT[:], in_=wsTf[:]), reads=[tsguc], writes=[tsguc])
            wk = 0
            for st in range(NST):
                r = 1 if st == 0 else 0
                t0 = st * TS
                for ti in range(2):
                    P.dma("sp", xa[ti][:], X[t0 + ti * 128:t0 + (ti + 1) * 128, :], reads=[tXs[st]], writes=[txa[ti]])
                    norm_modT(xa[ti], txa[ti], lambda c: Amix[:, l, c, r:r + 1], lambda c: modfm[:, l, 0 * 16 + c, r:r + 1],
                              hT, thT, slice(ti * 128, (ti + 1) * 128), tmp)
                for g in range(9):
                    b = wk % 3
                    wk += 1
                    wsrc = WIN[l, g].rearrange("p (k n) -> p k n", n=512)
                    P.dma("act", wA[b][:, 0:8, :], wsrc[:, 0:8, :], reads=[tW[("win", l, g)]], writes=[twA[b][0]])
                    P.dma("sp", wA[b][:, 8:16, :], wsrc[:, 8:16, :], reads=[tW[("win", l, g)]], writes=[twA[b][1]])
                    if g in (0, 1, 2, 3, 6, 8):
                        for blk in range(4):
                            bank = blk % 2
                            for kk in range(16):
                                P.op("pe", lambda h, b=b, kk=kk, blk=blk, bank=bank: h.matmul(
                                    PS[bank][:, 0:TS], lhsT=wA[b][:, kk, blk * 128:(blk + 1) * 128], rhs=hT[:, kk, :],
                                    start=(kk == 0), stop=(kk == 15)), reads=[twA[b][kk // 8], thT], writes=[tPS[bank]])
                            if g in (0, 1, 2, 3):
                                P.op("act", lambda h, blk=blk, bank=bank: h.copy(out=evb[:, blk, :], in_=PS[bank][:, 0:TS]),
                                     reads=[tPS[bank]], writes=[tev])
                            elif g == 6:
                                P.op("act", lambda h, blk=blk, bank=bank: h.activation(out=uT[:, blk, :], in_=PS[bank][:, 0:TS], func=AF.Gelu),
                                     reads=[tPS[bank]], writes=[tuT])
                            else:
                                P.op("act", lambda h, blk=blk, bank=bank: h.copy(out=evf[:, blk, :], in_=PS[bank][:, 0:TS]),
                                     reads=[tPS[bank]], writes=[tevf])
                        if g in (0, 1):
                            P.dma("sp", QT[g * 512:(g + 1) * 512, t0:t0 + TS].rearrange("(b p) t -> p b t", p=128), evb[:],
                                  reads=[tev], writes=[tQT])
                        elif g in (2, 3):
                            P.dma("sp", KT[(g - 2) * 512:(g - 1) * 512, t0:t0 + TS].rearrange("(b p) t -> p b t", p=128), evb[:],
                                  reads=[tev], writes=[tKT])
                        elif g == 8:
                            P.dma("sp", SINT[:, t0:t0 + TS].rearrange("(b p) t -> p b t", p=128), evf[:], reads=[tevf], writes=[tSINT])
                    else:
                        for ti in range(2):
                            bank = 2 + ti
                            for kk in range(16):
                                P.op("pe", lambda h, b=b, kk=kk, ti=ti, bank=bank: h.matmul(
                                    PS[bank][:, :], lhsT=hT[:, kk, ti * 128:(ti + 1) * 128], rhs=wA[b][:, kk, :],
                                    start=(kk == 0), stop=(kk == 15)), reads=[twA[b][kk // 8], thT], writes=[tPS[bank]])
                            if g in (4, 5):
                                P.op("act", lambda h, ti=ti, bank=bank, g=g: h.copy(out=vtok[:, ti, (g - 4) * 512:(g - 3) * 512], in_=PS[bank][:, :]),
                                     reads=[tPS[bank]], writes=[tvtok])
                            else:
                                P.op("act", lambda h, bank=bank: h.activation(out=svg[:], in_=PS[bank][:, :], func=AF.Gelu),
                                     reads=[tPS[bank]], writes=[tsv])
                                P.op("act", lambda h: h.activation(out=svq[:], in_=svg[:], func=AF.Square, accum_out=ss[:, 0:1]),
                                     reads=[tsv], writes=[tsv])
                                P.op("act", lambda h: h.activation(out=rt[:], in_=ss[:], func=AF.Sqrt, scale=1.0 / 512, bias=eps_t[:, 0:1]),
                                     reads=[tsv, tC], writes=[tsv])
                                P.op("dve", lambda h: h.reciprocal(out=rt[:], in_=rt[:]), reads=[tsv], writes=[tsv])
                                P.op("dve", lambda h, ti=ti: h.scalar_tensor_tensor(out=vsg[:, ti, :], in0=svg[:], scalar=rt[:, 0:1], in1=gsgu[:],
                                                                                   op0=ALU.mult, op1=ALU.mult), reads=[tsv, tsguc], writes=[tvsg])
                        if g == 5:
                            P.dma("sp", VV[t0:t0 + TS, :].rearrange("(i p) n -> p i n", p=128), vtok[:], reads=[tvtok], writes=[tVV])
                for ti in range(2):
                    for hh in range(4):
                        bank = 6 + (hh % 2)
                        P.op("pe", lambda h, ti=ti, hh=hh, bank=bank: h.matmul(PS[bank][:, 0:128], lhsT=vsg[:, ti, hh * 128:(hh + 1) * 128],
                                                                              rhs=wsT[:, hh, :], start=True, stop=True),
                             reads=[tvsg, tsguc], writes=[tPS[bank]])
                        P.op("dve", lambda h, hh=hh, bank=bank: h.tensor_tensor(out=sguT[:, hh, :], in0=PS[bank][:, 0:128], in1=bsb[:, hh, :], op=ALU.add),
                             reads=[tPS[bank], tsguc], writes=[tsguT])
                        P.op("dve", lambda h, hh=hh, ti=ti: h.tensor_tensor(out=sguT[:, hh, :], in0=sguT[:, hh, :], in1=uT[:, hh, ti * 128:(ti + 1) * 128],
                                                                           op=ALU.mult), reads=[tsguT, tuT], writes=[tsguT])
                        P.op("act", lambda h, hh=hh: h.activation(out=sqb[:, hh, :], in_=sguT[:, hh, :], func=AF.Square), reads=[tsguT], writes=[tsq])
                    for hh in range(4):
                        P.op("pe", lambda h, hh=hh: h.matmul(PS[1][:, 0:128], lhsT=ones_b[:], rhs=sqb[:, hh, :], start=(hh == 0), stop=(hh == 3)),
                             reads=[tsq, tC], writes=[tPS[1]])
                    rstd_bc_from_psum(1, 512, rsb, trs, 128)
                    for hh in range(4):
                        P.op("dve", lambda h, hh=hh: h.scalar_tensor_tensor(out=zsg[:, hh, :], in0=sguT[:, hh, :],
                                                                            scalar=vec_fm["out_norm_fm"][:, l, 8 + hh:9 + hh], in1=rsb[:],
                                                                            op0=ALU.mult, op1=ALU.mult), reads=[tsguT, trs, tMOD], writes=[tzs])
                    tt0 = t0 + ti * 128
                    P.dma("sp", ZT[1024:1536, tt0:tt0 + 128].rearrange("(b p) t -> p b t", p=128), zsg[:], reads=[tzs], writes=[tZT])
            P.flush()
        if cfg.stop_after == ("A", l):
            break
        with ExitStack() as es:
            S_ = lambda n, shp, dt: es.enter_context(nc.sbuf_tensor(f"{n}_{l}", shp, dt))
            NT = Tn // 128
            KTa = S_("KTa", [128, 8, Tn], BF16)
            Va = S_("Va", [128, NT, 1024], BF16)
            Qt = [S_("Qt0", [128, 8, 128], BF16), S_("Qt1", [128, 8, 128], BF16)]
            bT = [S_("bT0", [128, 5, 128], F32), S_("bT1", [128, 5, 128], F32)]
            tmpS = [S_("tmpS0", [128, 5, 128], F32), S_("tmpS1", [128, 5, 128], F32)]
            PT = [S_("PT0", [128, 7, 128], BF16), S_("PT1", [128, 7, 128], BF16)]
            rden = S_("rden", [128, 128], F32)
            attnT = S_("attnT", [128, 8, 128], F32)
            sqA = S_("sqA", [128, 8, 128], BF16)
            rsA = S_("rsA", [128, 128], F32)
            zA = S_("zA", [128, 8, 128], BF16)
            tKTa, tVa, trden, tattn, tsqA, trsA, tzA = [T() for _ in range(7)]
            tQt, tbT, ttmpS, tPT = [T(), T()], [T(), T()], [T(), T()], [T(), T()]
            for h_ in range(8):
                P.dma("sp", KTa[:, h_, :], KT[h_ * 128:(h_ + 1) * 128, :], reads=[tKT], writes=[tKTa])
            for i_ in range(NT):
                P.dma("act", Va[:, i_, :], VV[i_ * 128:(i_ + 1) * 128, :], reads=[tVV], writes=[tVa])
            sc_ = 128.0 ** -0.5
            it = 0
            qtiles = ([] if l == dep - 1 else [("ctx", 0), ("ctx", 1)]) + [("lat", q) for q in range(L // 128)]
            for qi, (kind, qn) in enumerate(qtiles):
                qb = qi % 2
                qtok = qn * 128 if kind == "ctx" else CTX + qn * 128
                P.dma("sp", Qt[qb][:], QT[:, qtok:qtok + 128].rearrange("(h d) t -> d h t", d=128), reads=[tQT], writes=[tQt[qb]])
                if kind == "lat":
                    first, nch, pat = na_info[qn]
                else:
                    first, nch, pat = 0, 0, 0
                for h_ in range(8):
                    p = it % 2
                    it += 1
                    bA, bB, bC = PS[2 * p], PS[2 * p + 1], PS[4 + p]
                    tA_, tB_, tC_ = tPS[2 * p], tPS[2 * p + 1], tPS[4 + p]
                    if nch:
                        P.dma("sp", bT[p][:, 0:nch, :], din["na_bias"][l, h_, pat, 0:nch].rearrange("c k q -> k c q"), writes=[tbT[p]])
                    for ci in range(nch):
                        kt0 = CTX + (first + ci) * 128
                        dst = bA[:, ci * 128:(ci + 1) * 128] if ci < 4 else bB[:, 0:128]
                        P.I("pe", "matmul", dst, lhsT=KTa[:, h_, kt0:kt0 + 128], rhs=Qt[qb][:, h_, :], start=True, stop=True,
                            reads=[tKTa, tQt[qb]], writes=[tA_ if ci < 4 else tB_])
                    for cc_ in range(2):
                        P.I("pe", "matmul", bB[:, 128 + cc_ * 128:256 + cc_ * 128], lhsT=KTa[:, h_, cc_ * 128:(cc_ + 1) * 128], rhs=Qt[qb][:, h_, :],
                            start=True, stop=True, reads=[tKTa, tQt[qb]], writes=[tB_])
                    if nch:
                        nA = min(nch, 4)
                        P.I("dve", "scalar_tensor_tensor", out=tmpS[p][:, 0:nA, :], in0=bA[:, 0:nA * 128].rearrange("p (c q) -> p c q", q=128), scalar=sc_,
                            in1=bT[p][:, 0:nA, :], op0=ALU.mult, op1=ALU.add, reads=[tA_, tbT[p]], writes=[ttmpS[p]])
                        if nch == 5:
                            P.I("dve", "scalar_tensor_tensor", out=tmpS[p][:, 4, :], in0=bB[:, 0:128], scalar=sc_,
                                in1=bT[p][:, 4, :], op0=ALU.mult, op1=ALU.add, reads=[tB_, tbT[p]], writes=[ttmpS[p]])
                        P.I("act", "activation", out=PT[p][:, 0:nch, :], in_=tmpS[p][:, 0:nch, :], func=AF.Exp, reads=[ttmpS[p]], writes=[tPT[p]])
                    P.I("act", "activation", out=PT[p][:, 5:7, :], in_=bB[:, 128:384].rearrange("p (c q) -> p c q", q=128), func=AF.Exp, scale=sc_,
                        reads=[tB_], writes=[tPT[p]])
                    chunks = [(CTX // 128 + first + ci, ci) for ci in range(nch)] + [(0, 5), (1, 6)]
                    for j_, (tokc, pc) in enumerate(chunks):
                        P.I("pe", "matmul", bC[:, 0:128], lhsT=Va[:, tokc, h_ * 128:(h_ + 1) * 128], rhs=PT[p][:, pc, :],
                            start=(j_ == 0), stop=(j_ == len(chunks) - 1), reads=[tVa, tPT[p]], writes=[tC_])
                    for j_, (tokc, pc) in enumerate(chunks):
                        P.I("pe", "matmul", bC[:, 128:256], lhsT=ones_b[:], rhs=PT[p][:, pc, :],
                            start=(j_ == 0), stop=(j_ == len(chunks) - 1), reads=[tC, tPT[p]], writes=[tC_])
                    P.I("dve", "reciprocal", out=rden[:], in_=bC[:, 128:256], reads=[tC_], writes=[trden])
                    P.I("dve", "tensor_tensor", out=attnT[:, h_, :], in0=bC[:, 0:128], in1=rden[:], op=ALU.mult, reads=[tC_, trden], writes=[tattn])
                    P.I("act", "activation", out=sqA[:, h_, :], in_=attnT[:, h_, :], func=AF.Square, reads=[tattn], writes=[tsqA])
                for h_ in range(8):
                    P.I("pe", "matmul", PS[6][:, 0:128], lhsT=ones_b[:], rhs=sqA[:, h_, :], start=(h_ == 0), stop=(h_ == 7),
                        reads=[tC, tsqA], writes=[tPS[6]])
                rstd_bc_from_psum(6, 1024, rsA, trsA, 128)
                for h_ in range(8):
                    P.I("dve", "scalar_tensor_tensor", out=zA[:, h_, :], in0=attnT[:, h_, :], scalar=vec_fm["out_norm_fm"][:, l, h_:h_ + 1], in1=rsA[:],
                        op0=ALU.mult, op1=ALU.mult, reads=[tattn, trsA, tMOD], writes=[tzA])
                P.dma("sp", ZT[0:1024, qtok:qtok + 128].rearrange("(b p) t -> p b t", p=128), zA[:], reads=[tzA], writes=[tZT])
            P.flush()
        if cfg.stop_after == ("B", l):
            break
        with ExitStack() as es:
            S_ = lambda n, shp, dt: es.enter_context(nc.sbuf_tensor(f"{n}_{l}", shp, dt))
            TWO_PI = 2.0 * math.pi
            MAGIC = 12582912.0
            hpi = S_("hpi", [128, 1], F32)
            tS = T()
            P.I("dve", "memset", hpi[:], math.pi / 2.0, writes=[tS])

            lm_cache = {}
            cx_cache = []

            def lam_math(tag, shp, src_are, src_aim, src_ldt):
                if tag not in lm_cache:
                    lm_cache[tag] = [S_(f"lm_{tag}_{k}", shp, F32) for k in range(10)]
                lr, li, dt_, rho, th, kk, ph, ab, cs, sn = lm_cache[tag]
                P.dma("sp", lr[:], src_are, writes=[tS])
                P.dma("sp", li[:], src_aim, writes=[tS])
                P.dma("sp", dt_[:], src_ldt, writes=[tS])
                P.I("dve", "tensor_scalar_min", out=lr[:], in0=lr[:], scalar1=-1e-4, reads=[tS], writes=[tS])
                P.I("act", "activation", out=dt_[:], in_=dt_[:], func=AF.Exp, reads=[tS], writes=[tS])
                P.I("dve", "tensor_tensor", out=rho[:], in0=lr[:], in1=dt_[:], op=ALU.mult, reads=[tS], writes=[tS])
                P.I("act", "activation", out=rho[:], in_=rho[:], func=AF.Exp, reads=[tS], writes=[tS])
                P.I("dve", "tensor_tensor", out=th[:], in0=li[:], in1=dt_[:], op=ALU.mult, reads=[tS], writes=[tS])
                P.I("dve", "tensor_scalar", out=kk[:], in0=th[:], scalar1=1.0 / TWO_PI, scalar2=MAGIC, op0=ALU.mult, op1=ALU.add, reads=[tS], writes=[tS])
                P.I("dve", "tensor_scalar", out=kk[:], in0=kk[:], scalar1=-MAGIC, scalar2=-TWO_PI, op0=ALU.add, op1=ALU.mult, reads=[tS], writes=[tS])
                P.I("dve", "tensor_tensor", out=ph[:], in0=th[:], in1=kk[:], op=ALU.add, reads=[tS], writes=[tS])
                P.I("dve", "tensor_scalar", out=ph[:], in0=ph[:], scalar1=-math.pi, scalar2=math.pi, op0=ALU.max, op1=ALU.min, reads=[tS], writes=[tS])
                P.I("act", "activation", out=sn[:], in_=ph[:], func=AF.Sin, reads=[tS], writes=[tS])
                P.I("dve", "tensor_scalar", out=ab[:], in0=ph[:], scalar1=-1.0, scalar2=None, op0=ALU.mult, reads=[tS], writes=[tS])
                P.I("dve", "tensor_tensor", out=ab[:], in0=ab[:], in1=ph[:], op=ALU.max, reads=[tS], writes=[tS])
                P.I("act", "activation", out=cs[:], in_=ab[:], func=AF.Sin, scale=-1.0, bias=hpi[:, 0:1], reads=[tS], writes=[tS])
                return lr, li, rho, cs, sn

            Ec = S_("Ec", [128, 16, TS], F32)
            Es = S_("Es", [128, 16, TS], F32)
            et = [S_(f"et{k}", [128, 16, 128], F32) for k in range(4)]
            BTr = S_("BTr", [128, 4, 128], BF16)
            BTi = S_("BTi", [128, 4, 128], BF16)
            BT3r = S_("BT3r", [128, 4, 128], BF16)
            BT3i = S_("BT3i", [128, 4, 128], BF16)
            CTr = S_("CTr", [128, 16, 128], BF16)
            CTi = S_("CTi", [128, 16, 128], BF16)
            ctmp = S_("ctmp", [128, 16, 128], F32)
            carry = S_("carry", [128, 2, 16], F32)
            yTf = S_("yTf", [128, 4, TS], F32)
            NB = 5
            rtN = [[S_(f"rt{k}_{u}", [128, TS], F32) for k in range(4)] for u in range(NB)]
            trtN = [[T() for _ in range(4)] for u in range(NB)]
            gN = [[S_(f"g{k}_{u}", [128, TS], F32) for k in range(2)] for u in range(NB)]
            GN = [[S_(f"G{k}_{u}", [128, TS], F32) for k in range(2)] for u in range(NB)]
            hN = [[S_(f"h{k}_{u}", [128, TS], F32) for k in range(2)] for u in range(NB)]
            tgN, tGN, thN = [T() for _ in range(NB)], [T() for _ in range(NB)], [T() for _ in range(NB)]
            uTf2 = [S_("uTf_a", [128, 4, TS], F32), S_("uTf_b", [128, 4, TS], F32)]
            uTb2 = [S_("uTb_a", [128, 4, TS], BF16), S_("uTb_b", [128, 4, TS], BF16)]
            tuTf2, tuTb2 = [T(), T()], [T(), T()]
            hbr = [S_("hbr0", [128, TS], BF16), S_("hbr1", [128, TS], BF16)]
            hbi = [S_("hbi0", [128, TS], BF16), S_("hbi1", [128, TS], BF16)]
            thb = [T(), T()]
            tE, tBC, tcar0, tuTf, tuTb, tyTf = [T() for _ in range(6)]
            tcars = [T() for _ in range(16)]
            for d_ in range(2):
                lr, li, rho, cs, sn = lam_math("sm", [128, 16], din["s5_are_sm"][l, d_], din["s5_aim_sm"][l, d_], din["s5_ldt_sm"][l, d_])
                P.I("dve", "tensor_copy", out=Ec[:, :, 0], in_=cs[:], reads=[tS], writes=[tE])
                P.I("dve", "tensor_copy", out=Es[:, :, 0], in_=sn[:], reads=[tS], writes=[tE])
                m_ = 1
                while m_ < TS:
                    bc_c = Ec[:, :, m_ - 1:m_].to_broadcast([128, 16, m_])
                    bc_s = Es[:, :, m_ - 1:m_].to_broadcast([128, 16, m_])
                    P.I("dve", "tensor_tensor", out=et[0][:, :, 0:m_], in0=Ec[:, :, 0:m_], in1=bc_c, op=ALU.mult, reads=[tE], writes=[tE])
                    P.I("dve", "tensor_tensor", out=et[1][:, :, 0:m_], in0=Es[:, :, 0:m_], in1=bc_s, op=ALU.mult, reads=[tE], writes=[tE])
                    P.I("dve", "tensor_tensor", out=et[2][:, :, 0:m_], in0=Ec[:, :, 0:m_], in1=bc_s, op=ALU.mult, reads=[tE], writes=[tE])
                    P.I("dve", "tensor_tensor", out=et[3][:, :, 0:m_], in0=Es[:, :, 0:m_], in1=bc_c, op=ALU.mult, reads=[tE], writes=[tE])
                    P.I("dve", "tensor_tensor", out=Ec[:, :, m_:2 * m_], in0=et[0][:, :, 0:m_], in1=et[1][:, :, 0:m_], op=ALU.subtract, reads=[tE], writes=[tE])
                    P.I("dve", "tensor_tensor", out=Es[:, :, m_:2 * m_], in0=et[2][:, :, 0:m_], in1=et[3][:, :, 0:m_], op=ALU.add, reads=[tE], writes=[tE])
                    m_ *= 2
                lrx, lix, rhox, csx, snx = lam_math("ce", [128, 4, 128], din["s5_are_ce"][l, d_], din["s5_aim_ce"][l, d_], din["s5_ldt_ce"][l, d_])
                if not cx_cache:
                    cx_cache.extend([S_(f"cx_{k}", [128, 4, 128], F32) for k in range(9)])
                lbr, lbi, nr, ni, den, t1, t2, bre, bim = cx_cache
                I_ = lambda *a, **k: P.I(*a, reads=[tS], writes=[tS], **k)
                I_("dve", "tensor_tensor", out=lbr[:], in0=rhox[:], in1=csx[:], op=ALU.mult)
                I_("dve", "tensor_scalar_add", out=lbr[:], in0=lbr[:], scalar1=-1.0)
                I_("dve", "tensor_tensor", out=lbi[:], in0=rhox[:], in1=snx[:], op=ALU.mult)
                I_("dve", "tensor_tensor", out=t1[:], in0=lbr[:], in1=lrx[:], op=ALU.mult)
                I_("dve", "tensor_tensor", out=t2[:], in0=lbi[:], in1=lix[:], op=ALU.mult)
                I_("dve", "tensor_tensor", out=nr[:], in0=t1[:], in1=t2[:], op=ALU.add)
                I_("dve", "tensor_tensor", out=t1[:], in0=lbi[:], in1=lrx[:], op=ALU.mult)
                I_("dve", "tensor_tensor", out=t2[:], in0=lbr[:], in1=lix[:], op=ALU.mult)
                I_("dve", "tensor_tensor", out=ni[:], in0=t1[:], in1=t2[:], op=ALU.subtract)
                I_("dve", "tensor_tensor", out=t1[:], in0=lrx[:], in1=lrx[:], op=ALU.mult)
                I_("dve", "tensor_tensor", out=t2[:], in0=lix[:], in1=lix[:], op=ALU.mult)
                I_("dve", "tensor_tensor", out=den[:], in0=t1[:], in1=t2[:], op=ALU.add)
                I_("dve", "reciprocal", out=den[:], in_=den[:])
                I_("dve", "tensor_tensor", out=nr[:], in0=nr[:], in1=den[:], op=ALU.mult)
                I_("dve", "tensor_tensor", out=ni[:], in0=ni[:], in1=den[:], op=ALU.mult)
                P.dma("sp", bre[:], din["s5_bre_ce"][l, d_], writes=[tS])
                P.dma("sp", bim[:], din["s5_bim_ce"][l, d_], writes=[tS])
                I_("dve", "tensor_tensor", out=t1[:], in0=nr[:], in1=bre[:], op=ALU.mult)
                I_("dve", "tensor_tensor", out=t2[:], in0=ni[:], in1=bim[:], op=ALU.mult)
                P.I("dve", "tensor_tensor", out=BTr[:], in0=t1[:], in1=t2[:], op=ALU.subtract, reads=[tS], writes=[tBC])
                I_("dve", "tensor_tensor", out=t1[:], in0=nr[:], in1=bim[:], op=ALU.mult)
                I_("dve", "tensor_tensor", out=t2[:], in0=ni[:], in1=bre[:], op=ALU.mult)
                P.I("dve", "tensor_tensor", out=BTi[:], in0=t1[:], in1=t2[:], op=ALU.add, reads=[tS], writes=[tBC])
                P.I("dve", "tensor_scalar", out=BT3r[:], in0=BTr[:], scalar1=pmask[:, 3:4], scalar2=None, op0=ALU.mult, reads=[tBC, tC], writes=[tBC])
                P.I("dve", "tensor_scalar", out=BT3i[:], in0=BTi[:], scalar1=pmask[:, 3:4], scalar2=None, op0=ALU.mult, reads=[tBC, tC], writes=[tBC])
                P.dma("sp", ctmp[:], din["s5_cre"][l, d_], writes=[tS])
                P.I("act", "copy", out=CTr[:], in_=ctmp[:], reads=[tS], writes=[tBC])
                P.dma("sp", ctmp[:], din["s5_cim"][l, d_], reads=[tBC], writes=[tS])
                P.I("act", "mul", out=CTi[:], in_=ctmp[:], mul=-1.0, reads=[tS], writes=[tBC])
                P.I("dve", "memset", carry[:], 0.0, writes=tcars)
                units = [(ck, c_, q_) for ck in range(NST) for c_ in range(4) for q_ in range(4)]
                NU = len(units)

                def chunk_tok(ck):
                    if d_ == 0:
                        return ck * TS, False
                    return (0 if ck == 0 else CTX + L - ck * TS), True

                def ph_load(ck):
                    ta, rev = chunk_tok(ck)
                    cb = ck % 2
                    P.dma("sp", uTf2[cb][:], SINT[:, ta:ta + TS].rearrange("(c p) t -> p c t", p=128), reads=[tSINT], writes=[tuTf2[cb]])
                    src = uTf2[cb][:, :, ::-1] if rev else uTf2[cb][:]
                    P.I("act", "copy", out=uTb2[cb][:], in_=src, reads=[tuTf2[cb]], writes=[tuTb2[cb]])

                def ph0(u):
                    ck, c_, q_ = units[u]
                    cb, pp = ck % 2, u % 2
                    bR, bI, tR, tI = PS[2 * pp], PS[2 * pp + 1], tPS[2 * pp], tPS[2 * pp + 1]
                    if q_ < 3:
                        lr_, li_, rsl = BTr[32 * q_:32 * q_ + 32, c_, :], BTi[32 * q_:32 * q_ + 32, c_, :], slice(32 * q_, 32 * q_ + 32)
                    else:
                        lr_, li_, rsl = BT3r[64:128, c_, :], BT3i[64:128, c_, :], slice(64, 128)
                    P.I("pe", "matmul", bR[:, 0:TS], lhsT=lr_, rhs=uTb2[cb][rsl, c_, :], start=True, stop=True, reads=[tBC, tuTb2[cb]], writes=[tR])
                    P.I("pe", "matmul", bI[:, 0:TS], lhsT=li_, rhs=uTb2[cb][rsl, c_, :], start=True, stop=True, reads=[tBC, tuTb2[cb]], writes=[tI])

                def ph1(u):
                    ck, c_, q_ = units[u]
                    i_, pp, sl = c_ * 4 + q_, u % 2, u % NB
                    bR, bI, tR, tI = PS[2 * pp], PS[2 * pp + 1], tPS[2 * pp], tPS[2 * pp + 1]
                    rt_, trt = rtN[sl], trtN[sl]
                    P.I("dve", "tensor_tensor", out=rt_[0][:], in0=bR[:, 0:TS], in1=Ec[:, i_, :], op=ALU.mult, reads=[tR, tE], writes=[trt[0]])
                    P.I("dve", "tensor_tensor", out=rt_[1][:], in0=bI[:, 0:TS], in1=Es[:, i_, :], op=ALU.mult, reads=[tI, tE], writes=[trt[1]])
                    P.I("dve", "tensor_tensor", out=rt_[2][:], in0=bI[:, 0:TS], in1=Ec[:, i_, :], op=ALU.mult, reads=[tI, tE], writes=[trt[2]])
                    P.I("dve", "tensor_tensor", out=rt_[3][:], in0=bR[:, 0:TS], in1=Es[:, i_, :], op=ALU.mult, reads=[tR, tE], writes=[trt[3]])

                def ph2(u):
                    sl = u % NB
                    rt_, trt = rtN[sl], trtN[sl]
                    P.I("pool", "tensor_tensor", out=gN[sl][0][:], in0=rt_[0][:], in1=rt_[1][:], op=ALU.add, reads=[trt[0], trt[1]], writes=[tgN[sl]])
                    P.I("pool", "tensor_tensor", out=gN[sl][1][:], in0=rt_[2][:], in1=rt_[3][:], op=ALU.subtract, reads=[trt[2], trt[3]], writes=[tgN[sl]])

                def ph3(u):
                    ck, c_, q_ = units[u]
                    i_, sl = c_ * 4 + q_, u % NB
                    for ri in range(2):
                        P.I("dve", "tensor_tensor_scan", out=GN[sl][ri][:], data0=rho[:, i_:i_ + 1].to_broadcast([128, TS]), data1=gN[sl][ri][:],
                            initial=carry[:, ri, i_:i_ + 1], op0=ALU.mult, op1=ALU.add, reads=[tgN[sl], tS, tcars[i_]], writes=[tGN[sl]])

                def ph4(u):
                    ck, c_, q_ = units[u]
                    i_, sl = c_ * 4 + q_, u % NB
                    rt_, trt = rtN[sl], trtN[sl]
                    P.I("pool", "tensor_tensor", out=rt_[0][:], in0=GN[sl][0][:], in1=Ec[:, i_, :], op=ALU.mult, reads=[tGN[sl], tE], writes=[trt[0]])
                    P.I("pool", "tensor_tensor", out=rt_[1][:], in0=GN[sl][1][:], in1=Es[:, i_, :], op=ALU.mult, reads=[tGN[sl], tE], writes=[trt[1]])
                    P.I("pool", "tensor_tensor", out=rt_[2][:], in0=GN[sl][0][:], in1=Es[:, i_, :], op=ALU.mult, reads=[tGN[sl], tE], writes=[trt[2]])
                    P.I("pool", "tensor_tensor", out=rt_[3][:], in0=GN[sl][1][:], in1=Ec[:, i_, :], op=ALU.mult, reads=[tGN[sl], tE], writes=[trt[3]])

                def ph5(u):
                    sl = u % NB
                    rt_, trt = rtN[sl], trtN[sl]
                    P.I("dve", "tensor_tensor", out=hN[sl][0][:], in0=rt_[0][:], in1=rt_[1][:], op=ALU.subtract, reads=[trt[0], trt[1]], writes=[thN[sl]])
                    P.I("dve", "tensor_tensor", out=hN[sl][1][:], in0=rt_[2][:], in1=rt_[3][:], op=ALU.add, reads=[trt[2], trt[3]], writes=[thN[sl]])

                def ph6(u):
                    ck, c_, q_ = units[u]
                    i_, sl, pp = c_ * 4 + q_, u % NB, u % 2
                    ta, rev = chunk_tok(ck)
                    yb = 4 + (c_ % 2)
                    P.I("act", "copy", out=carry[:, 0, i_:i_ + 1], in_=hN[sl][0][:, TS - 1:TS], reads=[thN[sl]], writes=[tcars[i_]])
                    P.I("act", "copy", out=carry[:, 1, i_:i_ + 1], in_=hN[sl][1][:, TS - 1:TS], reads=[thN[sl]], writes=[tcars[i_]])
                    P.I("act", "copy", out=hbr[pp][:], in_=hN[sl][0][:], reads=[thN[sl]], writes=[thb[pp]])
                    P.I("act", "copy", out=hbi[pp][:], in_=hN[sl][1][:], reads=[thN[sl]], writes=[thb[pp]])
                    P.I("pe", "matmul", PS[yb][:, 0:TS], lhsT=CTr[:, i_, :], rhs=hbr[pp][:], start=(q_ == 0), stop=False, reads=[tBC, thb[pp]], writes=[tPS[yb]])
                    P.I("pe", "matmul", PS[yb][:, 0:TS], lhsT=CTi[:, i_, :], rhs=hbi[pp][:], start=False, stop=(q_ == 3), reads=[tBC, thb[pp]], writes=[tPS[yb]])
                    if q_ == 3:
                        dst = yTf[:, c_, ::-1] if rev else yTf[:, c_, :]
                        P.I("act", "copy", out=dst, in_=PS[yb][:, 0:TS], reads=[tPS[yb]], writes=[tyTf])
                        if c_ == 3:
                            P.dma("sp", YS[d_, :, ta:ta + TS].rearrange("(c p) t -> p c t", p=128), yTf[:], reads=[tyTf], writes=[tYS])

                for t_ in range(NU + 6):
                    if t_ < NU:
                        if units[t_][1] == 0 and units[t_][2] == 0:
                            ph_load(units[t_][0])
                        ph0(t_)
                        ph1(t_)
                    if 0 <= t_ - 4 < NU:
                        ph5(t_ - 4)
                    if 0 <= t_ - 2 < NU:
                        ph3(t_ - 2)
                    if 0 <= t_ - 3 < NU:
                        ph4(t_ - 3)
                    if 0 <= t_ - 1 < NU:
                        ph2(t_ - 1)
                    if 0 <= t_ - 5 < NU:
                        ph6(t_ - 5)
            P.flush()
        if cfg.stop_after == ("C1", l):
            break
        with ExitStack() as es:
            S_ = lambda n, shp, dt: es.enter_context(nc.sbuf_tensor(f"{n}_{l}", shp, dt))
            GWf = S_("GWf", [128, 4, 512], F32)
            GW = S_("GW", [128, 4, 512], BF16)
            y0 = S_("y0", [128, 4, TS], F32)
            y1 = S_("y1", [128, 4, TS], F32)
            uu = S_("uu", [128, 4, TS], F32)
            yg = S_("yg", [128, 4, TS], F32)
            ygb = S_("ygb", [128, 4, TS], BF16)
            sg = S_("sg", [128, 4, TS], F32)
            sq2 = S_("sq2", [128, 4, TS], BF16)
            rs2 = S_("rs2", [128, TS], F32)
            z2 = S_("z2", [128, 4, TS], BF16)
            tGW, ty0, ty1, tuu, tyg, tygb, tsg, tsq2, trs2, tz2 = [T() for _ in range(10)]
            P.dma("sp", GWf[:], din["glu_w"][l].rearrange("(k p) n -> p k n", p=128), writes=[tGW])
            P.I("dve", "tensor_copy", out=GW[:], in_=GWf[:], reads=[tGW], writes=[tGW])
            for st in range(NST):
                if l == dep - 1 and st == 0:
                    continue
                t0 = st * TS
                P.dma("sp", y0[:], YS[0, :, t0:t0 + TS].rearrange("(c p) t -> p c t", p=128), reads=[tYS], writes=[ty0])
                P.dma("sp", y1[:], YS[1, :, t0:t0 + TS].rearrange("(c p) t -> p c t", p=128), reads=[tYS], writes=[ty1])
                P.dma("act", uu[:], SINT[:, t0:t0 + TS].rearrange("(c p) t -> p c t", p=128), reads=[tSINT], writes=[tuu])
                P.I("pool", "tensor_tensor", out=y0[:], in0=y0[:], in1=y1[:], op=ALU.add, reads=[ty0, ty1], writes=[ty0])
                for c_ in range(4):
                    P.I("dve", "scalar_tensor_tensor", out=y0[:, c_, :], in0=uu[:, c_, :], scalar=vec_fm["s5_d_fm"][:, l, c_:c_ + 1], in1=y0[:, c_, :],
                        op0=ALU.mult, op1=ALU.add, reads=[tuu, ty0, tMOD], writes=[ty0])
                P.I("act", "activation", out=yg[:], in_=y0[:], func=AF.Gelu, reads=[ty0], writes=[tyg])
                P.I("dve", "tensor_copy", out=ygb[:], in_=yg[:], reads=[tyg], writes=[tygb])
                for co in range(4):
                    bank = co % 2
                    for ki in range(4):
                        P.I("pe", "matmul", PS[bank][:, 0:TS], lhsT=GW[:, ki, co * 128:(co + 1) * 128], rhs=ygb[:, ki, :], start=(ki == 0), stop=(ki == 3),
                            reads=[tGW, tygb], writes=[tPS[bank]])
                    P.I("act", "activation", out=sg[:, co, :], in_=PS[bank][:, 0:TS], func=AF.Sigmoid, bias=vec_fm["glu_b_fm"][:, l, co:co + 1],
                        reads=[tPS[bank], tMOD], writes=[tsg])
                P.I("dve", "tensor_tensor", out=sg[:], in0=sg[:], in1=yg[:], op=ALU.mult, reads=[tsg, tyg], writes=[tsg])
                P.I("act", "activation", out=sq2[:], in_=sg[:], func=AF.Square, reads=[tsg], writes=[tsq2])
                for c_ in range(4):
                    P.I("pe", "matmul", PS[2][:, 0:TS], lhsT=ones_b[:], rhs=sq2[:, c_, :], start=(c_ == 0), stop=(c_ == 3), reads=[tC, tsq2], writes=[tPS[2]])
                rstd_bc_from_psum(2, 512, rs2, trs2, TS)
                for c_ in range(4):
                    P.I("dve", "scalar_tensor_tensor", out=z2[:, c_, :], in0=sg[:, c_, :], scalar=vec_fm["out_norm_fm"][:, l, 12 + c_:13 + c_], in1=rs2[:],
                        op0=ALU.mult, op1=ALU.mult, reads=[tsg, trs2, tMOD], writes=[tz2])
                P.dma("sp", ZT[1536:2048, t0:t0 + TS].rearrange("(b p) t -> p b t", p=128), z2[:], reads=[tz2], writes=[tZT])
            P.flush()
        if cfg.stop_after == ("C2", l):
            break
        with ExitStack() as es:
            S_ = lambda n, shp, dt: es.enter_context(nc.sbuf_tensor(f"{n}_{l}", shp, dt))
            ZTs = S_("ZTs", [128, 16, TS], BF16)
            wD = [S_("wD0", [128, 16, 512], BF16), S_("wD1", [128, 16, 512], BF16)]
            xd = [S_("xd0", [128, D], F32), S_("xd1", [128, D], F32)]
            gbc = [S_("gbc0", [128, D], F32), S_("gbc1", [128, D], F32)]
            tmpD = [S_("tmpD0", [128, 512], F32), S_("tmpD1", [128, 512], F32)]
            tZs, tgb_ = T(), T()
            twD, txd, ttmpD = [T(), T()], [T(), T()], [T(), T()]
            for r_ in range(2):
                P.dma("sp", gbc[r_][:], GATE[l, 0, r_], reads=[tGATE], writes=[tgb_])
            wk = 0
            for st in range(NST):
                if l == dep - 1 and st == 0:
                    continue
                r = 1 if st == 0 else 0
                t0 = st * TS
                P.dma("sp", ZTs[:], ZT[:, t0:t0 + TS].rearrange("(k p) t -> p k t", p=128), reads=[tZT], writes=[tZs])
                for ti in range(2):
                    P.dma("sp", xd[ti][:], X[t0 + ti * 128:t0 + (ti + 1) * 128, :], reads=[tXs[st]], writes=[txd[ti]])
                for cg in range(4):
                    b = wk % 2
                    wk += 1
                    P.dma("act", wD[b][:], WOUT[l, cg].rearrange("p (k n) -> p k n", n=512), reads=[tW[("wout", l, cg)]], writes=[twD[b]])
                    for ti in range(2):
                        bank = (cg * 2 + ti) % 4
                        for kk in range(16):
                            P.I("pe", "matmul", PS[bank][:, :], lhsT=ZTs[:, kk, ti * 128:(ti + 1) * 128], rhs=wD[b][:, kk, :], start=(kk == 0), stop=(kk == 15),
                                reads=[tZs, twD[b]], writes=[tPS[bank]])
                        P.I("dve", "tensor_tensor", out=tmpD[ti][:], in0=PS[bank][:, :], in1=gbc[r][:, cg * 512:(cg + 1) * 512], op=ALU.mult,
                            reads=[tPS[bank], tgb_], writes=[ttmpD[ti]])
                        P.I("pool", "tensor_tensor", out=xd[ti][:, cg * 512:(cg + 1) * 512], in0=xd[ti][:, cg * 512:(cg + 1) * 512], in1=tmpD[ti][:], op=ALU.add,
                            reads=[ttmpD[ti], txd[ti]], writes=[txd[ti]])
                for ti in range(2):
                    P.dma("sp", X[t0 + ti * 128:t0 + (ti + 1) * 128, :], xd[ti][:], reads=[txd[ti]], writes=[tXs[st]])
            P.flush()
        if cfg.stop_after == ("D", l):
            break
        with ExitStack() as es:
            S_ = lambda n, shp, dt: es.enter_context(nc.sbuf_tensor(f"{n}_{l}", shp, dt))
            xe = S_("xe", [128, D], F32)
            hTE = S_("hTE", [128, 16, TS], BF16)
            xnE = S_("xnE", [128, D], BF16)
            ssE = S_("ssE", [128, 1], F32)
            rtE = S_("rtE", [128, 1], F32)
            wE = [S_("wE0", [128, 16, 256], BF16), S_("wE1", [128, 16, 256], BF16)]
            qT = S_("qT", [128, 16, TS], BF16)
            k1b = S_("k1b", [128, 8, 128], BF16)
            k2b = S_("k2b", [128, 8, 128], BF16)
            s1 = [S_("s1_0", [128, 8, 128], F32), S_("s1_1", [128, 8, 128], F32)]
            s2 = [S_("s2_0", [128, 8, 128], F32), S_("s2_1", [128, 8, 128], F32)]
            v1 = S_("v1", [128, 8, 16], F32)
            v2 = S_("v2", [128, 8, 16], F32)
            best = S_("best", [128, 8, 16], F32)
            tmpm = S_("tmpm", [128, 256], F32)
            ebt = S_("ebt", [128, 8, 16], F32)
            zs = S_("zs", [128, 8], F32)
            cth = [S_("cth0", [128, 8], F32), S_("cth1", [128, 8], F32)]
            cthr = [S_("cthr0", [128, 8], F32), S_("cthr1", [128, 8], F32)]
            gel = [S_("gel0", [128, 8, TS], BF16), S_("gel1", [128, 8, TS], BF16)]
            NGB = 2
            Gb = [S_(f"Gb{i_}", [128, 8, 8, 128], BF16) for i_ in range(NGB)]
            tgel, tGb = [T(), T()], [T() for _ in range(NGB)]
            gbk = 0
            pending = None
            PrAll = S_("PrAll", [128, 2, 8, 128], F32)
            Pr = [PrAll[:, 0], PrAll[:, 1]]
            cand = PrAll[:].rearrange("p a h k -> p (a h k)").rearrange("p (h c) -> p h c", c=256)
            WT = S_("WT", [128, 128, TS], BF16)
            Vb = [S_("Vb0", [128, D], BF16), S_("Vb1", [128, D], BF16)]
            gE = S_("gE", [128, D], F32)
            tmpE = S_("tmpE", [128, 512], F32)
            txe, thTE, tqT, tk, tv, tcth, tWT, tgE, ttmpE = [T() for _ in range(9)]
            twE, ts1, ts2, tPr, tVb = [[T(), T()] for _ in range(5)]
            tmpN = (xnE, ssE, rtE, xnE, T())
            kf = cand[:, :, 0:128]
            P.dma("sp", kf, din["k1T"][l].rearrange("h d k -> d h k"), writes=[tv, tPr[0], tPr[1]])
            P.I("dve", "tensor_copy", out=k1b[:], in_=kf, reads=[tv], writes=[tk])
            P.dma("sp", kf, din["k2T"][l].rearrange("h d k -> d h k"), reads=[tk], writes=[tv])
            P.I("dve", "tensor_copy", out=k2b[:], in_=kf, reads=[tv], writes=[tk])
            wk = 0
            vk = 0
            gk = 0
            last_r = None
            def e_fixed(st):
                nonlocal wk
                r = 1 if st == 0 else 0
                t0 = st * TS
                for ti in range(2):
                    P.dma("sp", xe[:], X[t0 + ti * 128:t0 + (ti + 1) * 128, :], reads=[tXs[st]], writes=[txe])
                    norm_modT(xe, txe, lambda c, r=r: Affn[:, l, c, r:r + 1], lambda c, r=r: modfm[:, l, 3 * 16 + c, r:r + 1],
                              hTE, thTE, slice(ti * 128, (ti + 1) * 128), tmpN)
                for g in range(8):
                    b = wk % 2
                    wk += 1
                    P.dma("sp", wE[b][:], WQ[l, g].rearrange("p (k n) -> p k n", n=256), reads=[tW[("wq", l, g)]], writes=[twE[b]])
                    for blk in range(2):
                        bank = blk % 2
                        for kk in range(16):
                            P.I("pe", "matmul", PS[bank][:, 0:TS], lhsT=wE[b][:, kk, blk * 128:(blk + 1) * 128], rhs=hTE[:, kk, :], start=(kk == 0), stop=(kk == 15),
                                reads=[twE[b], thTE], writes=[tPS[bank]])
                        P.I("act", "copy", out=qT[:, g * 2 + blk, :], in_=PS[bank][:, 0:TS], reads=[tPS[bank]], writes=[tqT])
                for ti in range(2):
                    tsl = slice(ti * 128, (ti + 1) * 128)
                    for h_ in range(8):
                        P.I("pe", "matmul", PS[4 + h_ // 4][:, (h_ % 4) * 128:(h_ % 4 + 1) * 128], lhsT=qT[:, 2 * h_, tsl], rhs=k1b[:, h_, :], start=True, stop=True,
                            reads=[tqT, tk], writes=[tPS[4 + h_ // 4]])
                        P.I("pe", "matmul", PS[6 + h_ // 4][:, (h_ % 4) * 128:(h_ % 4 + 1) * 128], lhsT=qT[:, 2 * h_ + 1, tsl], rhs=k2b[:, h_, :], start=True, stop=True,
                            reads=[tqT, tk], writes=[tPS[6 + h_ // 4]])
                    for hb in range(2):
                        P.I("act", "copy", out=s1[ti][:, hb * 4:(hb + 1) * 4, :], in_=PS[4 + hb][:, :].rearrange("p (h k) -> p h k", k=128), reads=[tPS[4 + hb]], writes=[ts1[ti]])
                        P.I("act", "copy", out=s2[ti][:, hb * 4:(hb + 1) * 4, :], in_=PS[6 + hb][:, :].rearrange("p (h k) -> p h k", k=128), reads=[tPS[6 + hb]], writes=[ts2[ti]])
                    for (sx, tsx, vx) in ((s1[ti], ts1[ti], v1), (s2[ti], ts2[ti], v2)):
                        for h_ in range(8):
                            P.I("dve", "max", out=vx[:, h_, 0:8], in_=sx[:, h_, :], reads=[tsx], writes=[tv])
                            P.I("dve", "match_replace", out=tmpm[:, 0:128], in_to_replace=vx[:, h_, 0:8], in_values=sx[:, h_, :], imm_value=-1e30, reads=[tsx, tv], writes=[tv])
                            P.I("dve", "max", out=vx[:, h_, 8:16], in_=tmpm[:, 0:128], reads=[tv], writes=[tv])
                    P.I("dve", "tensor_tensor", out=cand[:].rearrange("p h (a b) -> p h a b", b=16), in0=v1[:].unsqueeze(3).to_broadcast([128, 8, 16, 16]),
                        in1=v2[:].unsqueeze(2).to_broadcast([128, 8, 16, 16]), op=ALU.add, reads=[tv], writes=[tv, tPr[0], tPr[1]])
                    for h_ in range(8):
                        P.I("dve", "max", out=best[:, h_, 0:8], in_=cand[:, h_, :], reads=[tv], writes=[tv])
                        P.I("dve", "match_replace", out=tmpm[:, :], in_to_replace=best[:, h_, 0:8], in_values=cand[:, h_, :], imm_value=-1e30, reads=[tv], writes=[tv])
                        P.I("dve", "max", out=best[:, h_, 8:16], in_=tmpm[:, :], reads=[tv], writes=[tv])
                    th_bc16 = best[:, :, 15:16].to_broadcast([128, 8, 16])
                    th_bc128 = best[:, :, 15:16].to_broadcast([128, 8, 128])
                    P.I("dve", "tensor_tensor", out=ebt[:], in0=best[:], in1=th_bc16, op=ALU.subtract, reads=[tv], writes=[tv])
                    P.I("act", "activation", out=ebt[:], in_=ebt[:], func=AF.Exp, reads=[tv], writes=[tv])
                    P.I("dve", "tensor_reduce", out=zs[:], in_=ebt[:], axis=mybir.AxisListType.X, op=ALU.add, reads=[tv], writes=[tv])
                    P.I("dve", "reciprocal", out=cth[ti][:], in_=zs[:], reads=[tv], writes=[tcth])
                    P.I("dve", "tensor_scalar", out=cthr[ti][:], in0=cth[ti][:], scalar1=0.9995, scalar2=None, op0=ALU.mult, reads=[tcth], writes=[tcth])
                    P.I("dve", "tensor_tensor", out=s1[ti][:], in0=s1[ti][:], in1=th_bc128, op=ALU.subtract, reads=[ts1[ti], tv], writes=[ts1[ti]])
                    P.I("act", "activation", out=s1[ti][:], in_=s1[ti][:], func=AF.Exp, reads=[ts1[ti]], writes=[ts1[ti]])
                    P.I("dve", "tensor_tensor", out=s1[ti][:], in0=s1[ti][:], in1=cth[ti][:].unsqueeze(2).to_broadcast([128, 8, 128]), op=ALU.mult,
                        reads=[ts1[ti], tcth], writes=[ts1[ti]])
                    P.I("act", "activation", out=s2[ti][:], in_=s2[ti][:], func=AF.Exp, reads=[ts2[ti]], writes=[ts2[ti]])

            def e_phase1(st):
                nonlocal wk, gk, gbk, pending
                for g in range(16):
                    gp = g % 2
                    for ub in range(4):
                        b = wk % 2
                        wk += 1
                        P.dma("sp", wE[b][:], UT[l, g * 4 + ub].rearrange("p (k n) -> p k n", n=256),
                              reads=[tW[("ut", l, g * 4 + ub)]], writes=[twE[b]])
                        for j in range(2):
                            blk = ub * 2 + j
                            bank = blk % 2
                            for kk in range(16):
                                P.I("pe", "matmul", PS[bank][:, 0:TS], lhsT=wE[b][:, kk, j * 128:(j + 1) * 128], rhs=hTE[:, kk, :], start=(kk == 0), stop=(kk == 15),
                                    reads=[thTE, twE[b]], writes=[tPS[bank]])
                            P.I("act", "activation", out=gel[gp][:, blk, :], in_=PS[bank][:, 0:TS], func=AF.Gelu, reads=[tPS[bank]], writes=[tgel[gp]])
                    for ti in range(2):
                        gb = gbk % NGB
                        gbk += 1
                        gbanks = (2, 3) if gb == 0 else (6, 7)
                        for h_ in range(8):
                            pb = gk % 2
                            gk += 1
                            pr_eng = "dve" if h_ % 4 == 3 else "pool"
                            P.I(pr_eng, "tensor_tensor", out=Pr[pb][:], in0=s1[ti][:, h_, 8 * g:8 * g + 8].unsqueeze(2).to_broadcast([128, 8, 128]),
                                in1=s2[ti][:, h_, :].unsqueeze(1).to_broadcast([128, 8, 128]), op=ALU.mult, reads=[ts1[ti], ts2[ti], tv], writes=[tPr[pb]])
                            P.I("dve", "scalar_tensor_tensor", out=Gb[gb][:, h_, :, :], in0=Pr[pb][:], scalar=cthr[ti][:, h_:h_ + 1], in1=Pr[pb][:], op0=ALU.is_ge, op1=ALU.mult,
                                reads=[tPr[pb], tcth], writes=[tGb[gb]])
                        if pending is not None:
                            pending()
                        for blk in range(8):
                            gbank = gbanks[blk // 4]
                            for h_ in range(8):
                                P.I("pe", "matmul", PS[gbank][:, (blk % 4) * 128:(blk % 4 + 1) * 128], lhsT=Gb[gb][:, h_, blk, :], rhs=ident_b[:], start=(h_ == 0), stop=(h_ == 7),
                                    reads=[tGb[gb], tC], writes=[tPS[gbank]])

                        def mk_pending(g=g, gp=gp, ti=ti, gbanks=gbanks):
                            def run():
                                for half in range(2):
                                    gbank = gbanks[half]
                                    P.I("dve", "tensor_tensor", out=WT[:, g * 8 + half * 4:g * 8 + half * 4 + 4, ti * 128:(ti + 1) * 128],
                                        in0=PS[gbank][:, :].rearrange("p (b t) -> p b t", t=128), in1=gel[gp][:, half * 4:half * 4 + 4, ti * 128:(ti + 1) * 128], op=ALU.mult,
                                        reads=[tPS[gbank], tgel[gp]], writes=[tWT])
                            return run
                        pending = mk_pending()
                if pending is not None:
                    pending()
                    pending = None

            def e_phase2(st):
                nonlocal vk, last_r
                r = 1 if st == 0 else 0
                t0 = st * TS
                if r != last_r:
                    P.dma("sp", gE[:], GATE[l, 1, r], reads=[tGATE], writes=[tgE])
                    last_r = r
                for blk in range(128):
                    b = vk % 2
                    vk += 1
                    P.dma("sp", Vb[b][:], VT[l, blk * 128:(blk + 1) * 128, :], reads=[tW[("vt", l)]], writes=[tVb[b]])
                    for ti in range(2):
                        for cg in range(4):
                            bank = ti * 4 + cg
                            P.I("pe", "matmul", PS[bank][:, :], lhsT=WT[:, blk, ti * 128:(ti + 1) * 128], rhs=Vb[b][:, cg * 512:(cg + 1) * 512], start=(blk == 0), stop=(blk == 127),
                                reads=[tWT, tVb[b]], writes=[tPS[bank]])
                for ti in range(2):
                    P.dma("sp", xe[:], X[t0 + ti * 128:t0 + (ti + 1) * 128, :], reads=[tXs[st]], writes=[txe])
                    for cg in range(4):
                        bank = ti * 4 + cg
                        P.I("dve", "tensor_tensor", out=tmpE[:], in0=PS[bank][:, :], in1=gE[:, cg * 512:(cg + 1) * 512], op=ALU.mult, reads=[tPS[bank], tgE], writes=[ttmpE])
                        P.I("pool", "tensor_tensor", out=xe[:, cg * 512:(cg + 1) * 512], in0=xe[:, cg * 512:(cg + 1) * 512], in1=tmpE[:], op=ALU.add,
                            reads=[ttmpE, txe], writes=[txe])
                    P.dma("sp", X[t0 + ti * 128:t0 + (ti + 1) * 128, :], xe[:], reads=[txe], writes=[tXs[st]])

            st_list = [st for st in range(NST) if not (l == dep - 1 and st == 0)]
            e_fixed(st_list[0])
            for si, st in enumerate(st_list):
                e_phase1(st)
                if si + 1 < len(st_list):
                    e_fixed(st_list[si + 1])
                e_phase2(st)
            P.flush()
        if cfg.stop_after == ("E", l):
            break
    if cfg.stop_after is None:
        with ExitStack() as es:
            S_ = lambda n, shp, dt: es.enter_context(nc.sbuf_tensor(f"{n}_{l}", shp, dt))
            gfin = S_("gfin", [128, D], F32)
            xf = [S_("xf0", [128, D], F32), S_("xf1", [128, D], F32)]
            of = [S_("of0", [128, D], F32), S_("of1", [128, D], F32)]
            jf = S_("jf", [128, D], BF16)
            ssf = [S_("ssf0", [128, 1], F32), S_("ssf1", [128, 1], F32)]
            tgf, tjf = T(), T()
            txf, tof, tssf = [T(), T()], [T(), T()], [T(), T()]
            P.dma("sp", gfin[:], din["final_norm"].partition_broadcast(128), writes=[tgf])
            for i_ in range(L // 128):
                b = i_ % 2
                tok = CTX + i_ * 128
                P.dma("sp", xf[b][:], X[tok:tok + 128, :], reads=[tXs[tok // TS]], writes=[txf[b]])
                P.I("act", "activation", out=jf[:], in_=xf[b][:], func=AF.Square, accum_out=ssf[b][:, 0:1], reads=[txf[b]], writes=[tjf, tssf[b]])
                P.I("act", "activation", out=ssf[b][:], in_=ssf[b][:], func=AF.Sqrt, scale=1.0 / D, bias=eps_t[:, 0:1], reads=[tssf[b], tC], writes=[tssf[b]])
                P.I("dve", "reciprocal", out=ssf[b][:], in_=ssf[b][:], reads=[tssf[b]], writes=[tssf[b]])
                P.I("dve", "scalar_tensor_tensor", out=of[b][:], in0=xf[b][:], scalar=ssf[b][:, 0:1], in1=gfin[:], op0=ALU.mult, op1=ALU.mult,
                    reads=[txf[b], tssf[b], tgf], writes=[tof[b]])
                P.dma("act", out_d[i_ * 128:(i_ + 1) * 128, :], of[b][:], reads=[tof[b]], writes=[tOUT])
            P.flush()
    P.wait_all("act", [tOUT])
    P.wait_all("sp", tXs + [tQT, tKT, tVV, tSINT, tZT, tYS, tGATE, tOUT])
    P.wait_all("pool", list(tW.values()))
    P.emit()
    return nc


def kernel(**inputs):
    cfg = Cfg(L=4096, depth=4)
    inp = {k: np.asarray(v) for k, v in inputs.items()}
    sh, info = prep_shared(inp, cfg)
    B = inp["x"].shape[0]
    in_maps = []
    for core in range(8):
        b = core % B
        m = dict(sh)
        m["x_in"] = np.ascontiguousarray(np.concatenate([inp["ctx"][b], inp["x"][b]], 0), dtype=np.float32)
        m["cc"] = np.ascontiguousarray(np.stack([inp["c"][b], inp["c_ctx"]], 0), dtype=np.float32)
        in_maps.append(m)
    shapes = {k: v.shape for k, v in in_maps[0].items()}
    nc = build(cfg, info, shapes)
    res = run_bass_kernel_spmd(nc, in_maps, core_ids=list(range(8)))
    out = np.stack([np.asarray(res.results[b]["out"], dtype=np.float32) for b in range(B)], 0)
    return out
```
